# Optimizing a Trainium2 kernel written in Bass

```python
import jax, jax.numpy as jnp
from jax import lax
import numpy as np

D_MODEL = 1024
BATCH = 8
SEQ = 4096
DEPTH = 2

BLOCK = 128
A_HEADS = 8
A_HEAD_DIM = 64
A_WIDTH = A_HEADS * A_HEAD_DIM
DILATED = ((128, 1), (512, 4), (2048, 16))
B_WIDTH = D_MODEL - A_WIDTH
SCONV_WIDTH = 3
IN_SPLITS = (A_WIDTH, A_WIDTH, A_WIDTH, B_WIDTH, B_WIDTH, B_WIDTH)
IN_WIDTH = sum(IN_SPLITS)
RET_HEADS = 4
RET_KDIM = D_MODEL // RET_HEADS
RET_VDIM = 2 * D_MODEL // RET_HEADS
RET_CHUNK = 128
ROPE_BASE = 10000.0
D_FF = 2816
FFN_CONV_WIDTH = 3
EPS = 1e-6
N_EVEN = (DEPTH + 1) // 2
N_ODD = DEPTH // 2

kernel_name = "hybrid_dilated_shortconv_retention_trunk"


def rms_norm(x, g):
    xf = x.astype(jnp.float32)
    y = xf * lax.rsqrt(jnp.mean(xf * xf, axis=-1, keepdims=True) + EPS)
    return (y * g.astype(jnp.float32)).astype(x.dtype)


def causal_dwconv(x, w):
    K = w.shape[0]
    S = x.shape[1]
    xp = jnp.pad(x, ((0, 0), (K - 1, 0), (0, 0)))
    y = xp[:, 0:S] * w[0]
    for j in range(1, K):
        y = y + xp[:, j:j + S] * w[j]
    return y


def banded_causal_attention(q, k, v, reach):
    L, hd = q.shape[-2], q.shape[-1]
    lead = q.shape[:-2]
    nb = -(-L // BLOCK)
    pad_end = nb * BLOCK - L
    zp = [(0, 0)] * len(lead)
    qb = jnp.pad(q, zp + [(0, pad_end), (0, 0)]).reshape(*lead, nb, BLOCK, hd)
    kp = jnp.pad(k, zp + [(BLOCK, pad_end), (0, 0)]).reshape(*lead, nb + 1, BLOCK, hd)
    vp = jnp.pad(v, zp + [(BLOCK, pad_end), (0, 0)]).reshape(*lead, nb + 1, BLOCK, hd)
    kb = jnp.concatenate([kp[..., :-1, :, :], kp[..., 1:, :, :]], axis=-2)
    vb = jnp.concatenate([vp[..., :-1, :, :], vp[..., 1:, :, :]], axis=-2)
    s = jnp.einsum('...nqd,...nkd->...nqk', qb.astype(jnp.float32), kb.astype(jnp.float32)) * (hd ** -0.5)
    blk = jnp.arange(nb)[:, None, None]
    qi = jnp.arange(BLOCK)[None, :, None]
    kj = jnp.arange(2 * BLOCK)[None, None, :]
    rel = BLOCK + qi - kj
    kpos = blk * BLOCK - BLOCK + kj
    mask = (rel >= 0) & (rel <= reach) & (kpos >= 0)
    s = jnp.where(mask, s, -jnp.inf)
    m = jnp.max(s, axis=-1, keepdims=True)
    p = jnp.exp(s - m)
    den = jnp.sum(p, axis=-1, keepdims=True)
    o = jnp.einsum('...nqk,...nkd->...nqd', p, vb.astype(jnp.float32)) / den
    lse = (m + jnp.log(den))[..., 0]
    o = o.reshape(*lead, nb * BLOCK, hd)[..., :L, :]
    lse = lse.reshape(*lead, nb * BLOCK)[..., :L]
    return o, lse


def dilated_attention(q, k, v):
    Bn, H, S, hd = q.shape
    outs, lses = [], []
    for window, dil in DILATED:
        L = S // dil
        split = lambda t: t.reshape(Bn, H, L, dil, hd).swapaxes(2, 3)
        o, lse = banded_causal_attention(split(q), split(k), split(v), window // dil)
        outs.append(o.swapaxes(2, 3).reshape(Bn, H, S, hd))
        lses.append(lse.swapaxes(2, 3).reshape(Bn, H, S))
    wts = jax.nn.softmax(jnp.stack(lses), axis=0)
    return jnp.sum(wts[..., None] * jnp.stack(outs), axis=0)


def even_mixer(xn, w_in, q_gain, k_gain, sconv_w, w_out):
    Bn, S, _ = xn.shape
    proj = xn @ w_in
    q, k, v, gate_b, gate_c, h = jnp.split(proj, np.cumsum(IN_SPLITS)[:-1].tolist(), axis=-1)
    heads = lambda t: t.reshape(Bn, S, A_HEADS, A_HEAD_DIM)
    q = rms_norm(heads(q), q_gain).transpose(0, 2, 1, 3)
    k = rms_norm(heads(k), k_gain).transpose(0, 2, 1, 3)
    v = heads(v).transpose(0, 2, 1, 3)
    a = dilated_attention(q, k, v).astype(xn.dtype).transpose(0, 2, 1, 3).reshape(Bn, S, A_WIDTH)
    b = gate_b * causal_dwconv(gate_c * h, sconv_w)
    return jnp.concatenate([a, b], axis=-1) @ w_out


def rotary(x, pos):
    half = x.shape[-1] // 2
    inv = ROPE_BASE ** (-jnp.arange(half, dtype=jnp.float32) / half)
    ang = pos[:, None] * inv[None, :]
    cos = jnp.cos(ang)[None, :, None, :]
    sin = jnp.sin(ang)[None, :, None, :]
    x1, x2 = x[..., :half], x[..., half:]
    return jnp.concatenate([x1 * cos - x2 * sin, x1 * sin + x2 * cos], axis=-1)


def chunkwise_retention(q, k, v):
    Bn, S, H, dk = q.shape
    dv = v.shape[-1]
    nc = S // RET_CHUNK
    log_g = jnp.log1p(-(2.0 ** (-5.0 - jnp.arange(H, dtype=jnp.float32))))
    i = jnp.arange(RET_CHUNK, dtype=jnp.float32)
    diff = i[:, None] - i[None, :]
    inner_decay = jnp.where(diff >= 0, jnp.exp(log_g[:, None, None] * jnp.maximum(diff, 0.0)), 0.0)
    q_decay = jnp.exp(log_g[:, None] * (i + 1.0))
    k_decay = jnp.exp(log_g[:, None] * (RET_CHUNK - 1.0 - i))
    chunk_decay = jnp.exp(log_g * RET_CHUNK)
    to_chunks = lambda t: t.reshape(Bn, nc, RET_CHUNK, H, t.shape[-1]).transpose(1, 0, 3, 2, 4)

    def step(state, qkv):
        qc, kc, vc = qkv
        scores = jnp.einsum('bhqd,bhkd->bhqk', qc, kc) * inner_decay
        o = (jnp.einsum('bhqk,bhke->bhqe', scores, vc)
             + jnp.einsum('bhqd,bhde->bhqe', qc, state) * q_decay[..., None])
        state = (state * chunk_decay[:, None, None]
                 + jnp.einsum('bhkd,bhke->bhde', kc * k_decay[..., None], vc))
        return state, o

    state0 = jnp.zeros((Bn, H, dk, dv), jnp.float32)
    _, o = lax.scan(step, state0, (to_chunks(q), to_chunks(k), to_chunks(v)))
    return o.transpose(1, 0, 3, 2, 4).reshape(Bn, S, H, dv)


def retention_mixer(xn, wq, wk, wv, wg, gn_gain, wo):
    Bn, S, _ = xn.shape
    pos = jnp.arange(S, dtype=jnp.float32)
    q = (xn @ wq).reshape(Bn, S, RET_HEADS, RET_KDIM).astype(jnp.float32)
    k = (xn @ wk).reshape(Bn, S, RET_HEADS, RET_KDIM).astype(jnp.float32) * (RET_KDIM ** -0.5)
    v = (xn @ wv).reshape(Bn, S, RET_HEADS, RET_VDIM).astype(jnp.float32)
    o = chunkwise_retention(rotary(q, pos), rotary(k, pos), v)
    mu = jnp.mean(o, axis=-1, keepdims=True)
    var = jnp.mean(jnp.square(o - mu), axis=-1, keepdims=True)
    o = (o - mu) * lax.rsqrt(var + EPS) * gn_gain.astype(jnp.float32)
    o = o.reshape(Bn, S, RET_HEADS * RET_VDIM).astype(xn.dtype)
    return (jax.nn.silu(xn @ wg) * o) @ wo


def conv_glu_ffn(xn, w_up, conv_w, w_down):
    u = causal_dwconv(xn @ w_up, conv_w)
    g, val = jnp.split(u, 2, axis=-1)
    return (jax.nn.silu(g) * val) @ w_down


def setup_inputs(seed: int = 0) -> dict:
    key = jax.random.key(seed)
    ks = jax.random.split(key, 24)
    nrm = lambda k, shape, s: jax.random.normal(k, shape, jnp.float32) * s
    gain = lambda k, shape: 1.0 + 0.02 * jax.random.normal(k, shape, jnp.float32)
    D = D_MODEL
    return {
        "x": jax.random.normal(ks[0], (BATCH, SEQ, D), jnp.float32),
        "even_norm": gain(ks[1], (N_EVEN, D)),
        "even_w_in": nrm(ks[2], (N_EVEN, D, IN_WIDTH), D ** -0.5),
        "even_q_gain": gain(ks[3], (N_EVEN, A_HEAD_DIM)),
        "even_k_gain": gain(ks[4], (N_EVEN, A_HEAD_DIM)),
        "even_sconv_w": nrm(ks[5], (N_EVEN, SCONV_WIDTH, B_WIDTH), SCONV_WIDTH ** -0.5),
        "even_w_out": nrm(ks[6], (N_EVEN, A_WIDTH + B_WIDTH, D), (A_WIDTH + B_WIDTH) ** -0.5),
        "odd_norm": gain(ks[7], (N_ODD, D)),
        "ret_wq": nrm(ks[8], (N_ODD, D, RET_HEADS * RET_KDIM), D ** -0.5),
        "ret_wk": nrm(ks[9], (N_ODD, D, RET_HEADS * RET_KDIM), D ** -0.5),
        "ret_wv": nrm(ks[10], (N_ODD, D, RET_HEADS * RET_VDIM), D ** -0.5),
        "ret_wg": nrm(ks[11], (N_ODD, D, RET_HEADS * RET_VDIM), D ** -0.5),
        "ret_gn_gain": gain(ks[12], (N_ODD, RET_HEADS, RET_VDIM)),
        "ret_wo": nrm(ks[13], (N_ODD, RET_HEADS * RET_VDIM, D), (RET_HEADS * RET_VDIM) ** -0.5),
        "ffn_norm": gain(ks[14], (DEPTH, D)),
        "ffn_w_up": nrm(ks[15], (DEPTH, D, 2 * D_FF), D ** -0.5),
        "ffn_conv_w": nrm(ks[16], (DEPTH, FFN_CONV_WIDTH, 2 * D_FF), FFN_CONV_WIDTH ** -0.5),
        "ffn_w_down": nrm(ks[17], (DEPTH, D_FF, D), D_FF ** -0.5),
    }


def reference(x, even_norm, even_w_in, even_q_gain, even_k_gain, even_sconv_w, even_w_out,
              odd_norm, ret_wq, ret_wk, ret_wv, ret_wg, ret_gn_gain, ret_wo,
              ffn_norm, ffn_w_up, ffn_conv_w, ffn_w_down):
    for l in range(DEPTH):
        if l % 2 == 0:
            e = l // 2
            x = x + even_mixer(rms_norm(x, even_norm[e]), even_w_in[e], even_q_gain[e],
                               even_k_gain[e], even_sconv_w[e], even_w_out[e])
        else:
            o = l // 2
            x = x + retention_mixer(rms_norm(x, odd_norm[o]), ret_wq[o], ret_wk[o], ret_wv[o],
                                    ret_wg[o], ret_gn_gain[o], ret_wo[o])
        x = x + conv_glu_ffn(rms_norm(x, ffn_norm[l]), ffn_w_up[l], ffn_conv_w[l], ffn_w_down[l])
    return x
```

```python
import numpy as np
import concourse.bass as bass
import concourse.mybir as mybir
from concourse.bass_utils import run_bass_kernel_spmd

F32 = mybir.dt.float32
BF16 = mybir.dt.bfloat16
AF = mybir.ActivationFunctionType
ALU = mybir.AluOpType
AX = mybir.AxisListType

D = 1024
S = 4096
NT = 512
DFF = 2816
EPS = 1e-6
SB_BASE = 18432
SB_END = 229376


class Buf:
    __slots__ = ("w", "r", "const")

    def __init__(self, const=False):
        self.w = []
        self.r = []
        self.const = const


class Op:
    __slots__ = ("eng", "fn", "deps", "signal", "dsem", "dval", "val", "idx", "dprev")

    def __init__(self, eng, fn, deps):
        self.eng = eng
        self.fn = fn
        self.deps = deps
        self.signal = False
        self.dsem = None
        self.dval = 0
        self.val = 0


class DSem:
    def __init__(self):
        self.count = 0
        self.sem = None


ENGS = ("pe", "act", "dve", "pool", "sp")


class Sched:
    def __init__(self, nc, n_dma_sems):
        self.nc = nc
        self.q = {e: [] for e in ENGS}
        self.dpool = {e: [DSem() for _ in range(n)] for e, n in n_dma_sems.items()}
        self.dnext = {e: 0 for e in n_dma_sems}
        self.all_dma = []

    def _mk(self, eng, fn, reads, writes, deps, is_dma=False):
        d = list(deps)
        for b in reads:
            d.extend(b.w)
        for b in writes:
            d.extend(b.w)
            d.extend(b.r)
        o = Op(eng, fn, d)
        for b in reads:
            if not b.const:
                if not is_dma:
                    b.r = [x for x in b.r if not (x.eng == eng and x.dsem is None)]
                b.r.append(o)
        for b in writes:
            b.w = [o]
            b.r = []
        for x in d:
            if x.dsem is None and not (x.eng == "pe" and eng == "pe"):
                x.signal = True
        o.idx = len(self.q[eng])
        self.q[eng].append(o)
        return o

    def op(self, eng, fn, reads=(), writes=(), deps=()):
        return self._mk(eng, fn, reads, writes, deps)

    def dma(self, eng, out, in_, reads=(), writes=(), deps=(), track=True):
        pool = self.dpool[eng]
        ds = pool[self.dnext[eng] % len(pool)]
        self.dnext[eng] += 1
        o = self._mk(eng, lambda e: e.dma_start(out=out, in_=in_), reads, writes, deps, is_dma=True)
        o.dsem = ds
        o.dval = ds.count + 16
        o.dprev = ds.count
        ds.count += 16
        if track:
            self.all_dma.append(o)
        return o

    def barrier(self):
        lasts = []
        for e in ENGS:
            for o in reversed(self.q[e]):
                if o.dsem is None and o.fn is not None:
                    lasts.append(o)
                    break
        dm = list(self.all_dma)
        self.all_dma = []
        for e in ENGS:
            self._mk(e, None, (), (), lasts + dm)

    def emit(self, stack):
        nc = self.nc
        esem = {}
        for e in ("pe", "act", "dve", "pool"):
            esem[e] = stack.enter_context(nc.semaphore("prog_" + e))
        for e, pool in self.dpool.items():
            for i, ds in enumerate(pool):
                ds.sem = stack.enter_context(nc.semaphore("dma_%s_%d" % (e, i)))
        for e in ENGS:
            run = 0
            for o in self.q[e]:
                if o.dsem is None and o.signal:
                    run += 1
                    o.val = run
        self.nsig = {e: sum(1 for o in self.q[e] if o.signal and o.dsem is None) for e in ENGS}
        for sm in list(esem.values()) + [ds.sem for pool in self.dpool.values() for ds in pool]:
            nc.gpsimd.sem_clear(sm)
        nc.all_engine_barrier()
        block = stack.enter_context(nc.Block())
        q = self.q

        def run_engine(ename, eng):
            waited = {}

            def wait(sem, val):
                if val <= 0:
                    return
                k = sem.name
                if waited.get(k, 0) >= val:
                    return
                eng.wait_ge(sem, val)
                waited[k] = val

            for o in q[ename]:
                for d in o.deps:
                    if d.dsem is not None:
                        wait(d.dsem.sem, d.dval)
                    else:
                        if d.eng == "pe" and ename == "pe":
                            continue
                        wait(esem[d.eng], d.val)
                if o.dsem is not None:
                    wait(o.dsem.sem, o.dprev)
                if o.fn is None:
                    continue
                ins = o.fn(eng)
                if o.dsem is not None:
                    ins.then_inc(o.dsem.sem, 16)
                elif o.signal:
                    ins.then_inc(esem[ename], 1)

        @block.tensor
        def _(e):
            run_engine("pe", e)

        @block.scalar
        def _(e):
            run_engine("act", e)

        @block.vector
        def _(e):
            run_engine("dve", e)

        @block.gpsimd
        def _(e):
            run_engine("pool", e)

        @block.sync
        def _(e):
            run_engine("sp", e)


class SbAlloc:
    def __init__(self, nc):
        self.nc = nc
        self.persist = SB_BASE
        self.cur = SB_BASE
        self.n = 0

    def alloc(self, shape, dtype, persist=False):
        nbytes = int(np.prod(shape[1:])) * (4 if dtype == F32 else 2)
        nbytes = (nbytes + 63) // 64 * 64
        self.n += 1
        if persist:
            assert self.cur == self.persist, "persistent allocs must come first"
            off = self.persist
            self.persist += nbytes
            self.cur = self.persist
        else:
            off = self.cur
            self.cur += nbytes
        assert self.cur <= SB_END, "SBUF overflow: %d" % self.cur
        return self.nc.alloc_sbuf_tensor_at("sb%d" % self.n, list(shape), dtype, offset=off)

    def reset(self):
        self.cur = self.persist


class Ring:
    def __init__(self, items):
        self.items = items
        self.i = 0

    def next(self):
        x = self.items[self.i % len(self.items)]
        self.i += 1
        return x


class TB:
    def __init__(self, t, nb=1):
        self.t = t
        self.bs = [Buf() for _ in range(nb)]
        self.b = self.bs[0]


class Ctx:
    pass


def emit_rmsnorm_tile(cx, xt, gain_ap_fn, xn, nrm_div, ps_ss):
    S_ = cx.S
    sq = cx.sq
    S_.op("act", lambda e: e.activation(out=sq.t[:], in_=xt.t[:], func=AF.Square),
          reads=xt.bs, writes=[sq.b])
    for c in range(8):
        S_.op("pe", lambda e, c=c: e.matmul(ps_ss.t[:], cx.ones.t[:], sq.t[:, c, :],
                                             start=(c == 0), stop=(c == 7)),
              reads=[sq.b, cx.ones.b], writes=[ps_ss.b])
    rs = cx.rstd.next()
    S_.op("act", lambda e: e.activation(out=rs.t[:], in_=ps_ss.t[:], func=AF.Ln,
                                        bias=cx.epsc.t[:, 0:1], scale=1.0 / nrm_div),
          reads=[ps_ss.b, cx.epsc.b], writes=[rs.b])
    S_.op("act", lambda e: e.activation(out=rs.t[:], in_=rs.t[:], func=AF.Exp, scale=-0.5),
          reads=[rs.b], writes=[rs.b])
    for c in range(8):
        S_.op("dve", lambda e, c=c: e.scalar_tensor_tensor(
            out=xn.t[:, c, :], in0=xt.t[:, c, :], scalar=gain_ap_fn(c), in1=rs.t[:],
            op0=ALU.mult, op1=ALU.mult),
            reads=[xt.bs[c], rs.b, cx.consts_b], writes=[xn.bs[c]])


def emit_ffn(cx, xin, xout, wup, wdn, l, ntiles, wbufs=(), after_tile0=None, pre=None):
    S_ = cx.S
    A = cx.sb
    A.reset()
    xin_v = xin.rearrange("(c p) s -> p c s", p=128)
    xout_v = xout.rearrange("(c p) s -> p c s", p=128)

    xts = Ring([TB(A.alloc([128, 8, NT], F32), nb=8) for _ in range(2)])
    cx.sq = TB(A.alloc([128, 8, NT], BF16))
    cx.rstd = Ring([TB(A.alloc([128, NT], F32)) for _ in range(1)])
    xns = Ring([TB(A.alloc([128, 8, NT], BF16), nb=8) for _ in range(2)])
    wups = Ring([TB(A.alloc([128, 8, 256], BF16)) for _ in range(3)])
    wdns = Ring([TB(A.alloc([128, 22, 128], BF16)) for _ in range(3)])
    ubs = Ring([TB(A.alloc([128, 2, NT + 2], F32), nb=2) for _ in range(2)])
    ys = Ring([TB(A.alloc([128, 2, NT], F32), nb=2) for _ in range(3)])
    sgs = Ring([TB(A.alloc([128, NT], F32)) for _ in range(2)])
    hs = Ring([TB(A.alloc([128, 22, NT], BF16), nb=22) for _ in range(2)])
    carry = TB(A.alloc([128, 22, 2, 2], F32), nb=22)
    pups = Ring([(cx.psum[0], cx.psum[1]), (cx.psum[2], cx.psum[3])])
    pos = Ring([cx.psum[4], cx.psum[5]])
    ps_ss = cx.psum[6]

    S_.op("pool", lambda e: e.memset(carry.t[:], 0.0), writes=carry.bs)
    if pre is not None:
        act_v = pre["actT"].rearrange("(c p) s -> p c s", p=128)
        ats = Ring([TB(A.alloc([128, 8, NT], BF16)) for _ in range(2)])
        wos = Ring([TB(A.alloc([128, 8, 128], BF16)) for _ in range(2)])

    cw = cx.ffn_cw.t

    def down(xt, h, ti, ocs=range(8), store=True):
        for oc in ocs:
            wd = wdns.next()
            S_.dma("sp", wd.t[:], wdn[oc], reads=list(wbufs), writes=[wd.b])
            po = pos.next()
            for kc in range(22):
                S_.op("pe", lambda e, kc=kc, wd=wd, po=po: e.matmul(
                    po.t[:], wd.t[:, kc, :], h.t[:, kc, :], start=(kc == 0), stop=(kc == 21)),
                    reads=[wd.b, h.bs[kc]], writes=[po.b])
            S_.op("dve", lambda e, oc=oc, po=po: e.tensor_tensor(
                out=xt.t[:, oc, :], in0=po.t[:], in1=xt.t[:, oc, :], op=ALU.add),
                reads=[po.b], writes=[xt.bs[oc]])
        if store:
            S_.dma("pool", xout_v[:, :, ti * NT:(ti + 1) * NT], xt.t[:], reads=xt.bs)

    def gate(y, j, h):
        sg = sgs.next()
        S_.op("act", lambda e, y=y, sg=sg: e.activation(out=sg.t[:], in_=y.t[:, 0, :], func=AF.Silu),
              reads=[y.bs[0]], writes=[sg.b])
        S_.op("dve", lambda e, y=y, sg=sg, j=j, h=h: e.tensor_tensor(
            out=h.t[:, j, :], in0=sg.t[:], in1=y.t[:, 1, :], op=ALU.mult),
            reads=[sg.b, y.bs[1]], writes=[h.bs[j]])

    pend_gate = None
    prev = None
    for ti in range(ntiles):
        xt = xts.next()
        S_.dma("sp", xt.t[:], xin_v[:, :, ti * NT:(ti + 1) * NT], writes=xt.bs)
        if pre is not None:
            at = ats.next()
            S_.dma("sp", at.t[:], act_v[:, :, ti * NT:(ti + 1) * NT], writes=[at.b])
            for oc in range(8):
                wo_ = wos.next()
                S_.dma("sp", wo_.t[:], pre["wo"][oc], reads=[pre["wbuf"]], writes=[wo_.b])
                po = pos.next()
                for kc in range(8):
                    S_.op("pe", lambda e, kc=kc, wo_=wo_, po=po, at=at: e.matmul(
                        po.t[:], wo_.t[:, kc, :], at.t[:, kc, :], start=(kc == 0), stop=(kc == 7)),
                        reads=[wo_.b, at.b], writes=[po.b])
                S_.op("dve", lambda e, oc=oc, po=po, xt=xt: e.tensor_tensor(
                    out=xt.t[:, oc, :], in0=po.t[:], in1=xt.t[:, oc, :], op=ALU.add),
                    reads=[po.b], writes=[xt.bs[oc]])
            if prev is not None:
                down(*prev, ocs=range(0, 4), store=False)
        xn = xns.next()
        emit_rmsnorm_tile(cx, xt, lambda c: cx.ffn_g.t[:, l, c:c + 1], xn, float(D), ps_ss)
        if prev is not None:
            if pre is not None:
                down(*prev, ocs=range(4, 8), store=True)
            else:
                down(*prev)
        h = hs.next()
        for j in range(22):
            wu = wups.next()
            S_.dma("sp", wu.t[:], wup[j], reads=list(wbufs), writes=[wu.b])
            pg, pv = pups.next()
            for half, pp in ((0, pg), (1, pv)):
                for kc in range(8):
                    S_.op("pe", lambda e, kc=kc, half=half, pp=pp, wu=wu, xn=xn: e.matmul(
                        pp.t[:], wu.t[:, kc, half * 128:(half + 1) * 128], xn.t[:, kc, :],
                        start=(kc == 0), stop=(kc == 7)),
                        reads=[wu.b, xn.bs[kc]], writes=[pp.b])
            ub = ubs.next()
            y = ys.next()
            S_.op("act", lambda e, ub=ub, j=j: e.activation(
                out=ub.t[:, :, 0:2], in_=carry.t[:, j, :, :], func=AF.Copy),
                reads=[carry.bs[j]], writes=ub.bs)
            for half, pp in ((0, pg), (1, pv)):
                ch = j + 22 * half
                S_.op("act", lambda e, half=half, pp=pp, ub=ub: e.activation(
                    out=ub.t[:, half, 2:NT + 2], in_=pp.t[:], func=AF.Copy),
                    reads=[pp.b], writes=[ub.bs[half]])
                S_.op("act", lambda e, half=half, pp=pp, y=y, ch=ch: e.activation(
                    out=y.t[:, half, :], in_=pp.t[:], func=AF.Identity, scale=cw[:, l, ch, 2:3]),
                    reads=[pp.b, cx.consts_b], writes=[y.bs[half]])
            S_.op("act", lambda e, ub=ub, j=j: e.activation(
                out=carry.t[:, j, :, :], in_=ub.t[:, :, NT:NT + 2], func=AF.Copy),
                reads=ub.bs, writes=[carry.bs[j]])
            for k in (1, 0):
                for half in (0, 1):
                    ch = j + 22 * half
                    S_.op("dve", lambda e, half=half, ub=ub, y=y, ch=ch, k=k: e.scalar_tensor_tensor(
                        out=y.t[:, half, :], in0=ub.t[:, half, k:k + NT], scalar=cw[:, l, ch, k:k + 1],
                        in1=y.t[:, half, :], op0=ALU.mult, op1=ALU.add),
                        reads=[ub.bs[half], cx.consts_b], writes=[y.bs[half]])
            if pend_gate is not None:
                gate(*pend_gate)
            pend_gate = (y, j, h)
        gate(*pend_gate)
        pend_gate = None
        prev = (xt, h, ti)
        if after_tile0 is not None:
            after_tile0(ti)
    down(*prev)


def MM(S_, out, lhsT, rhs, start, stop, reads, writes):
    return S_.op("pe", lambda e: e.matmul(out, lhsT, rhs, start=start, stop=stop), reads, writes)


def TR(S_, out, in_, ident, reads, writes):
    return S_.op("pe", lambda e: e.transpose(out, in_, ident), reads, writes)


def ACT(S_, out, in_, func, reads, writes, **kw):
    return S_.op("act", lambda e: e.activation(out=out, in_=in_, func=func, **kw), reads, writes)


def TT(S_, eng, out, in0, in1, op, reads, writes):
    return S_.op(eng, lambda e: e.tensor_tensor(out=out, in0=in0, in1=in1, op=op), reads, writes)


def STT(S_, eng, out, in0, scalar, in1, op0, op1, reads, writes):
    return S_.op(eng, lambda e: e.scalar_tensor_tensor(out=out, in0=in0, scalar=scalar, in1=in1,
                                                       op0=op0, op1=op1), reads, writes)


def TS(S_, eng, out, in0, s1, s2, op0, op1, reads, writes):
    return S_.op(eng, lambda e: e.tensor_scalar(out=out, in0=in0, scalar1=s1, scalar2=s2,
                                                op0=op0, op1=op1), reads, writes)


def CP(S_, eng, out, in_, reads, writes):
    return S_.op(eng, lambda e: e.tensor_copy(out=out, in_=in_), reads, writes)


def MEMSET(S_, eng, out, val, writes):
    return S_.op(eng, lambda e: e.memset(out, val), (), writes)


def emit_l0_inproj(cx, xin, W, ntiles, after_tile0=None):
    S_ = cx.S
    A = cx.sb
    A.reset()
    C = cx.C
    xin_v = xin.rearrange("(c p) s -> p c s", p=128)
    xts = Ring([TB(A.alloc([128, 8, NT], F32), nb=8) for _ in range(2)])
    cx.sq = TB(A.alloc([128, 8, NT], BF16))
    cx.rstd = Ring([TB(A.alloc([128, NT], F32)) for _ in range(2)])
    xns = Ring([TB(A.alloc([128, 8, NT], BF16), nb=8) for _ in range(2)])
    wr = Ring([TB(A.alloc([128, 8, 128], BF16)) for _ in range(8)])
    wv = TB(A.alloc([128, 8, 512], BF16))
    sqh = Ring([TB(A.alloc([128, NT], BF16)) for _ in range(2)])
    rsh = Ring([TB(A.alloc([128, NT], F32)) for _ in range(2)])
    ost = Ring([TB(A.alloc([128, NT], BF16)) for _ in range(3)])
    vst = Ring([TB(A.alloc([128, 4, 2, 128], BF16)) for _ in range(3)])
    for vs_ in vst.items:
        MEMSET(S_, "pool", vs_.t[:], 0.0, [vs_.b])
    tmpf = Ring([TB(A.alloc([128, NT], F32)) for _ in range(2)])
    chs = Ring([TB(A.alloc([128, NT + 2], F32)) for _ in range(2)])
    ybs = Ring([TB(A.alloc([128, NT], F32)) for _ in range(2)])
    carry = TB(A.alloc([128, 4, 2], F32), nb=4)
    pr = Ring([cx.psum[0], cx.psum[1], cx.psum[2], cx.psum[3], cx.psum[5], cx.psum[7]])
    ps_h = cx.psum[4]
    ps_ss = cx.psum[6]
    MEMSET(S_, "pool", carry.t[:], 0.0, carry.bs)
    S_.dma("sp", wv.t[:], W["l0_wv"], reads=[W["b_l0"]], writes=[wv.b])
    sw = cx.sconv.t

    def proj(oc, xn):
        w = wr.next()
        S_.dma("sp", w.t[:], W["l0_wfm"][oc], reads=[W["bl_l0_wfm"][oc]], writes=[w.b])
        p = pr.next()
        for kc in range(8):
            MM(S_, p.t[:], w.t[:, kc, :], xn.t[:, kc, :], kc == 0, kc == 7, [w.b, xn.bs[kc]], [p.b])
        return p

    for ti in range(ntiles):
        tsl = slice(ti * NT, (ti + 1) * NT)
        xt = xts.next()
        S_.dma("sp", xt.t[:], xin_v[:, :, tsl], writes=xt.bs)
        xn = xns.next()
        emit_rmsnorm_tile(cx, xt, lambda c: cx.gains.t[:, 0, c:c + 1], xn, float(D), ps_ss)
        def finish(oc, p, sq):
            MM(S_, ps_h.t[:], cx.bdones.t[:], sq.t[:], True, True, [sq.b, cx.cb], [ps_h.b])
            rs = rsh.next()
            ACT(S_, rs.t[:], ps_h.t[:], AF.Ln, [ps_h.b, cx.cb], [rs.b], bias=cx.epsc.t[:, 0:1], scale=1.0 / 64)
            ACT(S_, rs.t[:], rs.t[:], AF.Exp, [rs.b], [rs.b], scale=-0.5)
            o = ost.next()
            gi = 0 if oc < 4 else 1
            STT(S_, "dve", o.t[:], p.t[:], cx.qkg.t[:, gi:gi + 1], rs.t[:], ALU.mult, ALU.mult,
                [p.b, rs.b, cx.cb], [o.b])
            dst = W["qT0"] if oc < 4 else W["kT0"]
            r0 = (oc % 4) * 128
            S_.dma("pool", dst[r0:r0 + 128, tsl], o.t[:], reads=[o.b])

        pend = None
        for oc in range(8):
            p = proj(oc, xn)
            sq = sqh.next()
            ACT(S_, sq.t[:], p.t[:], AF.Square, [p.b], [sq.b])
            if pend is not None:
                finish(*pend)
            pend = (oc, p, sq)
        finish(*pend)
        for sub in range(4):
            p = pr.next()
            for kc in range(8):
                MM(S_, p.t[:], xn.t[:, kc, sub * 128:(sub + 1) * 128], wv.t[:, kc, :], kc == 0, kc == 7,
                   [wv.b, xn.bs[kc]], [p.b])
            o = vst.next()
            pv4 = p.t[:].rearrange("p (c h e) -> p c h e", c=4, h=2)
            for hh in range(2):
                ACT(S_, o.t[:, :, hh, hh * 64:(hh + 1) * 64], pv4[:, :, hh, :], AF.Copy, [p.b], [o.b])
            r0 = ti * NT + sub * 128
            S_.dma("pool", W["v0p"][r0:r0 + 128, :], o.t[:].rearrange("p c h e -> p (c h e)"), reads=[o.b])
        for i in range(4):
            pgb = proj(8 + i, xn)
            pgc = proj(12 + i, xn)
            ph = proj(16 + i, xn)
            tf = tmpf.next()
            ACT(S_, tf.t[:], pgc.t[:], AF.Copy, [pgc.b], [tf.b])
            ch = chs.next()
            ACT(S_, ch.t[:, 0:2], carry.t[:, i, :], AF.Copy, [carry.bs[i]], [ch.b])
            TT(S_, "dve", ch.t[:, 2:NT + 2], tf.t[:], ph.t[:], ALU.mult, [tf.b, ph.b], [ch.b])
            ACT(S_, carry.t[:, i, :], ch.t[:, NT:NT + 2], AF.Copy, [ch.b], [carry.bs[i]])
            yb = ybs.next()
            ACT(S_, yb.t[:], ch.t[:, 2:NT + 2], AF.Identity, [ch.b, cx.cb], [yb.b], scale=sw[:, i, 2:3])
            STT(S_, "dve", yb.t[:], ch.t[:, 1:NT + 1], sw[:, i, 1:2], yb.t[:], ALU.mult, ALU.add,
                [ch.b, cx.cb], [yb.b])
            STT(S_, "dve", yb.t[:], ch.t[:, 0:NT], sw[:, i, 0:1], yb.t[:], ALU.mult, ALU.add,
                [ch.b, cx.cb], [yb.b])
            o = ost.next()
            TT(S_, "dve", o.t[:], yb.t[:], pgb.t[:], ALU.mult, [yb.b, pgb.b], [o.b])
            r0 = 512 + i * 128
            S_.dma("pool", W["abT"][r0:r0 + 128, tsl], o.t[:], reads=[o.b])
        if after_tile0 is not None:
            after_tile0(ti)


def emit_l0_attn(cx, W, Sx, after_pair0=None, after_pair=None, wo_staged=False):
    S_ = cx.S
    A = cx.sb
    A.reset()
    qpad = TB(A.alloc([128, 2, Sx], BF16))
    kT = TB(A.alloc([128, Sx], BF16))
    acc = TB(A.alloc([128, 2, Sx], F32))
    NBMAX = Sx // 128
    vbs = Ring([TB(A.alloc([128, NBMAX, 2, 128], BF16)) for _ in range(3)])
    pts = Ring([TB(A.alloc([128, 2, 256], BF16)) for _ in range(5)])
    outs = Ring([TB(A.alloc([128, NT], BF16)) for _ in range(2)])
    rcp = Ring([TB(A.alloc([128, NT], F32)) for _ in range(2)])
    pss = Ring([cx.psum[0], cx.psum[1], cx.psum[2], cx.psum[3]])
    pnd = Ring([cx.psum[4], cx.psum[5], cx.psum[6], cx.psum[7]])
    MEMSET(S_, "pool", qpad.t[:], 0.0, [qpad.b])
    wos_ = None
    if wo_staged:
        wos_ = WoStaged(cx, W)
        wos_.load(0)
    for c in range(4):
        S_.dma("sp", qpad.t[0:64, 0, :], W["qT0"][c * 128:c * 128 + 64, :], writes=[qpad.b])
        S_.dma("sp", qpad.t[64:128, 1, :], W["qT0"][c * 128 + 64:(c + 1) * 128, :], writes=[qpad.b])
        S_.dma("sp", kT.t[:], W["kT0"][c * 128:(c + 1) * 128, :], writes=[kT.b])
        MEMSET(S_, "pool", acc.t[:], 0.0, [acc.b])
        blocks = []
        for (win, d) in ((128, 1), (512, 4), (2048, 16)):
            L = Sx // d
            nb = L // 128
            for r in range(d):
                vb = vbs.next()
                src = W["v0p"][r:Sx:d, c * 256:(c + 1) * 256].rearrange("(kb j) e -> j kb e", j=128)
                loads = []
                for g0 in range(0, nb, 8):
                    g1 = min(nb, g0 + 8)
                    loads.append((vb.t[:, g0:g1, :, :].rearrange("p k h e -> p k (h e)"), src[:, g0:g1, :]))
                for kb in range(nb):
                    blocks.append((d, r, kb, nb, vb, loads if kb == 0 else None))

        def stage1(blk):
            d, r, kb, nb, vb, loads = blk
            if loads is not None:
                for (o_, i_) in loads:
                    S_.dma("sp", o_, i_, writes=[vb.b])
            nq = 256 if kb + 1 < nb else 128
            k0 = r + d * 128 * kb
            ksl = slice(k0, k0 + d * 127 + 1, d)
            qsl = slice(k0, k0 + d * (nq - 1) + 1, d)
            ps = pss.next()
            psv = ps.t[:, 0:2 * nq].rearrange("p (a n) -> p a n", a=2)
            MM(S_, psv, cx.ident.t[:], cx.mbias2.t[:, :, 0:nq], True, False, [cx.cb], [ps.b])
            MM(S_, psv, kT.t[:, ksl], qpad.t[:, :, qsl], False, True, [kT.b, qpad.b], [ps.b])
            pt = pts.next()
            ACT(S_, pt.t[:, :, 0:nq], psv, AF.Exp, [ps.b], [pt.b], scale=0.125)
            return (pt, nq, qsl, vb, kb)

        def stage2(st):
            pt, nq, qsl, vb, kb = st
            pnd_ = pnd.next()
            for hh in range(2):
                MM(S_, pnd_.t[:, 0:nq], vb.t[:, kb, hh, :], pt.t[:, hh, 0:nq], hh == 0, hh == 1,
                   [vb.b, pt.b], [pnd_.b])
            for hh in range(2):
                MM(S_, pnd_.t[:, 256:256 + nq], cx.onespad.t[:, hh, :], pt.t[:, hh, 0:nq], hh == 0, hh == 1,
                   [cx.cb, pt.b], [pnd_.b])
            TT(S_, "dve", acc.t[:, :, qsl], acc.t[:, :, qsl],
               pnd_.t[:].rearrange("p (a n) -> p a n", a=2)[:, :, 0:nq], ALU.add, [pnd_.b], [acc.b])

        LOOK = 2
        pend = []
        for blk in blocks:
            pend.append(stage1(blk))
            if len(pend) > LOOK:
                stage2(pend.pop(0))
        while pend:
            stage2(pend.pop(0))
        if c == 0 and after_pair0 is not None:
            after_pair0()
        if after_pair is not None:
            after_pair(c)
        if wos_ is not None:
            wos_.ops(c)
            if c + 1 < 4:
                wos_.load(c + 1)
        for ti in range(Sx // NT):
            tsl = slice(ti * NT, (ti + 1) * NT)
            rc = rcp.next()
            S_.op("dve", lambda e, o_=rc.t[:], i_=acc.t[:, 1, tsl]: e.reciprocal(out=o_, in_=i_), [acc.b], [rc.b])
            o = outs.next()
            TT(S_, "dve", o.t[:], acc.t[:, 0, tsl], rc.t[:], ALU.mult, [acc.b, rc.b], [o.b])
            S_.dma("pool", W["abT"][c * 128:(c + 1) * 128, tsl], o.t[:], reads=[o.b])


def emit_outproj(cx, xin, xout, actT, wo, wbuf, KC, ntiles):
    S_ = cx.S
    A = cx.sb
    A.reset()
    xin_v = xin.rearrange("(c p) s -> p c s", p=128)
    xout_v = xout.rearrange("(c p) s -> p c s", p=128)
    act_v = actT.rearrange("(c p) s -> p c s", p=128)
    xts = Ring([TB(A.alloc([128, 8, NT], F32), nb=8) for _ in range(2)])
    ats = Ring([TB(A.alloc([128, KC, NT], BF16)) for _ in range(2)])
    ws = Ring([TB(A.alloc([128, KC, 128], BF16)) for _ in range(3)])
    pr = Ring([cx.psum[0], cx.psum[1], cx.psum[2], cx.psum[3]])
    for ti in range(ntiles):
        tsl = slice(ti * NT, (ti + 1) * NT)
        xt = xts.next()
        S_.dma("sp", xt.t[:], xin_v[:, :, tsl], writes=xt.bs)
        at = ats.next()
        S_.dma("sp", at.t[:], act_v[:, :, tsl], writes=[at.b])
        for oc in range(8):
            w = ws.next()
            S_.dma("sp", w.t[:], wo[oc], reads=[wbuf], writes=[w.b])
            p = pr.next()
            for kc in range(KC):
                MM(S_, p.t[:], w.t[:, kc, :], at.t[:, kc, :], kc == 0, kc == KC - 1, [w.b, at.b], [p.b])
            TT(S_, "dve", xt.t[:, oc, :], p.t[:], xt.t[:, oc, :], ALU.add, [p.b], [xt.bs[oc]])
        S_.dma("pool", xout_v[:, :, tsl], xt.t[:], reads=xt.bs)


def emit_ret_proj(cx, xin, W, ntiles, after_tile0=None):
    S_ = cx.S
    A = cx.sb
    A.reset()
    xin_v = xin.rearrange("(c p) s -> p c s", p=128)
    Sx = ntiles * NT
    cos = TB(A.alloc([128, Sx], F32))
    sin = TB(A.alloc([128, Sx], F32))
    S_.dma("sp", cos.t[:], W["cos"][:, 0:Sx], writes=[cos.b])
    S_.dma("sp", sin.t[:], W["sin"][:, 0:Sx], writes=[sin.b])
    xts = Ring([TB(A.alloc([128, 8, NT], F32), nb=8) for _ in range(2)])
    cx.sq = TB(A.alloc([128, 8, NT], BF16))
    cx.rstd = Ring([TB(A.alloc([128, NT], F32)) for _ in range(2)])
    xns = Ring([TB(A.alloc([128, 8, NT], BF16), nb=8) for _ in range(2)])
    wr = Ring([TB(A.alloc([128, 8, 128], BF16)) for _ in range(4)])
    wsl = Ring([TB(A.alloc([128, 8, 512], BF16)) for _ in range(3)])
    t1s = Ring([TB(A.alloc([128, NT], F32)) for _ in range(3)])
    t2s = Ring([TB(A.alloc([128, NT], F32)) for _ in range(3)])
    t3s = Ring([TB(A.alloc([128, NT], F32)) for _ in range(3)])
    t4s = Ring([TB(A.alloc([128, NT], F32)) for _ in range(3)])
    ob = Ring([TB(A.alloc([128, NT], BF16)) for _ in range(6)])
    kTt = Ring([TB(A.alloc([128, 8, NT], BF16), nb=8) for _ in range(2)])
    kds = Ring([TB(A.alloc([128, 1024], BF16)) for _ in range(2)])
    obt = Ring([TB(A.alloc([128, 512], BF16)) for _ in range(3)])
    obf = Ring([TB(A.alloc([128, 512], F32)) for _ in range(2)])
    pr = Ring([cx.psum[0], cx.psum[1], cx.psum[2], cx.psum[3], cx.psum[4], cx.psum[7]])
    ptr = cx.psT
    ps_ss = cx.psum[6]

    def proj(oc, xn):
        w = wr.next()
        S_.dma("sp", w.t[:], W["r_wqk"][oc], reads=[W["b_rqk"]], writes=[w.b])
        p = pr.next()
        for kc in range(8):
            MM(S_, p.t[:], w.t[:, kc, :], xn.t[:, kc, :], kc == 0, kc == 7, [w.b, xn.bs[kc]], [p.b])
        return p

    for ti in range(ntiles):
        tsl = slice(ti * NT, (ti + 1) * NT)
        xt = xts.next()
        S_.dma("sp", xt.t[:], xin_v[:, :, tsl], writes=xt.bs)
        xn = xns.next()
        emit_rmsnorm_tile(cx, xt, lambda c: cx.gains.t[:, 1, c:c + 1], xn, float(D), ps_ss)
        kTb = kTt.next()
        for isk in (0, 1):
            for hd in range(4):
                p1 = proj(isk * 8 + 2 * hd, xn)
                p2 = proj(isk * 8 + 2 * hd + 1, xn)
                t1, t2, t3, t4 = t1s.next(), t2s.next(), t3s.next(), t4s.next()
                TT(S_, "dve", t1.t[:], p1.t[:], cos.t[:, tsl], ALU.mult, [p1.b, cos.b], [t1.b])
                TT(S_, "dve", t2.t[:], p2.t[:], sin.t[:, tsl], ALU.mult, [p2.b, sin.b], [t2.b])
                TT(S_, "dve", t3.t[:], p1.t[:], sin.t[:, tsl], ALU.mult, [p1.b, sin.b], [t3.b])
                TT(S_, "dve", t4.t[:], p2.t[:], cos.t[:, tsl], ALU.mult, [p2.b, cos.b], [t4.b])
                TT(S_, "dve", t1.t[:], t1.t[:], t2.t[:], ALU.subtract, [t2.b], [t1.b])
                TT(S_, "dve", t4.t[:], t3.t[:], t4.t[:], ALU.add, [t3.b], [t4.b])
                for half, rr in ((0, t1), (1, t4)):
                    ch = 2 * hd + half
                    r0 = ch * 128
                    if isk == 0:
                        o = ob.next()
                        ACT(S_, o.t[:], rr.t[:], AF.Copy, [rr.b], [o.b])
                        S_.dma("pool", W["r_qT"][r0:r0 + 128, tsl], o.t[:], reads=[o.b])
                        o2 = ob.next()
                        TT(S_, "dve", o2.t[:], rr.t[:], cx.qdec.t[:, hd, :], ALU.mult, [rr.b, cx.cb], [o2.b])
                        S_.dma("pool", W["r_qdT"][r0:r0 + 128, tsl], o2.t[:], reads=[o2.b])
                    else:
                        ACT(S_, kTb.t[:, ch, :], rr.t[:], AF.Copy, [rr.b], [kTb.bs[ch]], scale=1.0 / 16.0)
                        S_.dma("pool", W["r_kT"][r0:r0 + 128, tsl], kTb.t[:, ch, :], reads=[kTb.bs[ch]])
        for sub in range(4):
            for ch in range(8):
                TR(S_, ptr.t[:, ch * 128:(ch + 1) * 128], kTb.t[:, ch, sub * 128:(sub + 1) * 128], cx.ident.t[:],
                   [kTb.bs[ch], cx.cb], [ptr.b])
            kd = kds.next()
            for hd in range(4):
                S_.op("dve", lambda e, o_=kd.t[:, hd * 256:(hd + 1) * 256], i_=ptr.t[:, hd * 256:(hd + 1) * 256],
                      s_=cx.kdec.t[:, hd:hd + 1]: e.tensor_scalar_mul(out=o_, in0=i_, scalar1=s_),
                      [ptr.b, cx.cb], [kd.b])
            r0 = ti * NT + sub * 128
            S_.dma("pool", W["r_kd"][r0:r0 + 128, :], kd.t[:], reads=[kd.b])
        for which in (0, 1):
            for slab in range(4):
                w = wsl.next()
                S_.dma("sp", w.t[:], W["r_wvg"][which * 4 + slab], reads=[W["b_rvg"]], writes=[w.b])
                for sub in range(4):
                    p = pr.next()
                    for kc in range(8):
                        MM(S_, p.t[:], xn.t[:, kc, sub * 128:(sub + 1) * 128], w.t[:, kc, :], kc == 0, kc == 7,
                           [w.b, xn.bs[kc]], [p.b])
                    r0 = ti * NT + sub * 128
                    if which == 0:
                        o = obt.next()
                        ACT(S_, o.t[:], p.t[:], AF.Copy, [p.b], [o.b])
                        S_.dma("pool", W["r_v"][r0:r0 + 128, slab * 512:(slab + 1) * 512], o.t[:], reads=[o.b])
                    else:
                        o = obf.next()
                        ACT(S_, o.t[:], p.t[:], AF.Silu, [p.b], [o.b])
                        S_.dma("pool", W["r_g"][r0:r0 + 128, slab * 512:(slab + 1) * 512], o.t[:], reads=[o.b])
        if after_tile0 is not None:
            after_tile0(ti)


def emit_wo_scale(cx, W, reset=True):
    S_ = cx.S
    A = cx.sb
    if reset:
        A.reset()
    wf = Ring([TB(A.alloc([128, 16, 128], F32)) for _ in range(2)])
    wb = Ring([TB(A.alloc([128, 16, 128], BF16)) for _ in range(2)])
    for oc in range(8):
        f = wf.next()
        S_.dma("sp", f.t[:], W["r_wo_f"][oc], writes=[f.b])
        b = wb.next()
        for kc in range(16):
            S_.op("dve", lambda e, o_=b.t[:, kc, :], i_=f.t[:, kc, :], s_=cx.gng.t[:, kc:kc + 1]:
                  e.tensor_scalar_mul(out=o_, in0=i_, scalar1=s_), [f.b, cx.cb], [b.b])
        S_.dma("pool", W["r_wo"][oc], b.t[:], reads=[b.b], writes=[W["b_rwo"]])


class WoStaged:
    def __init__(self, cx, W):
        self.cx, self.W = cx, W
        A = cx.sb
        self.f = [TB(A.alloc([128, 16, 128], F32)) for _ in range(2)]
        self.b = [TB(A.alloc([128, 16, 128], BF16)) for _ in range(2)]

    def load(self, step):
        S_ = self.cx.S
        for i in range(2):
            S_.dma("sp", self.f[i].t[:], self.W["r_wo_f"][2 * step + i], writes=[self.f[i].b])

    def ops(self, step):
        S_, cx, W = self.cx.S, self.cx, self.W
        for i in range(2):
            f, b = self.f[i], self.b[i]
            for kc in range(16):
                S_.op("dve", lambda e, o_=b.t[:, kc, :], i_=f.t[:, kc, :], s_=cx.gng.t[:, kc:kc + 1]:
                      e.tensor_scalar_mul(out=o_, in0=i_, scalar1=s_), [f.b, cx.cb], [b.b])
            S_.dma("pool", W["r_wo"][2 * step + i], b.t[:], reads=[b.b], writes=[W["b_rwo"]])


def emit_ret_core(cx, xin, xout, W, nchunks, after_chunk=None):
    S_ = cx.S
    A = cx.sb
    A.reset()
    xin_v = xin.rearrange("(c p) s -> p c s", p=128)
    xout_v = xout.rearrange("(c p) s -> p c s", p=128)
    qT_v = W["r_qT"].rearrange("(c p) s -> p c s", p=128)
    qdT_v = W["r_qdT"].rearrange("(c p) s -> p c s", p=128)
    kT_v = W["r_kT"].rearrange("(c p) s -> p c s", p=128)
    wo = TB(A.alloc([128, 8, 16, 128], BF16))
    for oc in range(8):
        S_.dma("sp", wo.t[:, oc, :, :], W["r_wo"][oc], reads=[W["b_rwo"]], writes=[wo.b])
    stf = TB(A.alloc([128, 4, 2, 512], F32), nb=8)
    stb = TB(A.alloc([128, 4, 2, 512], BF16), nb=8)
    MEMSET(S_, "pool", stf.t[:], 0.0, stf.bs)
    MEMSET(S_, "pool", stb.t[:], 0.0, stb.bs)
    qs = Ring([TB(A.alloc([128, 8, 128], BF16)) for _ in range(2)])
    qds = Ring([TB(A.alloc([128, 8, 128], BF16)) for _ in range(2)])
    ks = Ring([TB(A.alloc([128, 8, 128], BF16)) for _ in range(2)])
    kds = Ring([TB(A.alloc([128, 1024], BF16)) for _ in range(2)])
    vs = Ring([TB(A.alloc([128, 2048], BF16)) for _ in range(2)])
    gs = Ring([TB(A.alloc([128, 2048], F32)) for _ in range(2)])
    xts = Ring([TB(A.alloc([128, 8, 128], F32)) for _ in range(2)])
    pts = Ring([TB(A.alloc([128, 4, 128], BF16)) for _ in range(2)])
    ofs = Ring([TB(A.alloc([128, 4, 512], F32), nb=4) for _ in range(2)])
    junk = TB(A.alloc([128, 512], F32))
    st1 = Ring([TB(A.alloc([128, 4], F32)) for _ in range(2)])
    st2 = Ring([TB(A.alloc([128, 4], F32)) for _ in range(2)])
    nmean = Ring([TB(A.alloc([128, 4], F32)) for _ in range(2)])
    var = Ring([TB(A.alloc([128, 4], F32)) for _ in range(2)])
    tmpy = Ring([TB(A.alloc([128, 512], F32)) for _ in range(2)])
    ytok = Ring([TB(A.alloc([128, 2048], BF16), nb=4) for _ in range(2)])
    yT = Ring([TB(A.alloc([128, 16, 128], BF16), nb=2) for _ in range(2)])
    ps_s = Ring([cx.psum[0]])
    ps_o = Ring([cx.psum[1], cx.psum[2]])
    ps_st = Ring([cx.psum[3], cx.psum[4]])
    ps_tr = cx.psT
    ps_op = Ring([cx.psum[6], cx.psum[7]])
    pending_tail = None
    for c in range(nchunks):
        csl = slice(c * 128, (c + 1) * 128)
        q, qd, k, kd, v, g, xt = qs.next(), qds.next(), ks.next(), kds.next(), vs.next(), gs.next(), xts.next()
        S_.dma("sp", q.t[:], qT_v[:, :, csl], writes=[q.b])
        S_.dma("sp", qd.t[:], qdT_v[:, :, csl], writes=[qd.b])
        S_.dma("sp", k.t[:], kT_v[:, :, csl], writes=[k.b])
        S_.dma("sp", kd.t[:], W["r_kd"][csl, :], writes=[kd.b])
        S_.dma("sp", v.t[:], W["r_v"][csl, :], writes=[v.b])
        S_.dma("sp", g.t[:], W["r_g"][csl, :], writes=[g.b])
        S_.dma("sp", xt.t[:], xin_v[:, :, csl], writes=[xt.b])
        of = ofs.next()
        s1, s2 = st1.next(), st2.next()
        MEMSET(S_, "dve", s1.t[:], 0.0, [s1.b])
        MEMSET(S_, "dve", s2.t[:], 0.0, [s2.b])
        pS = ps_s.next()
        for hd in range(4):
            MM(S_, pS.t[:, hd * 128:(hd + 1) * 128], k.t[:, 2 * hd, :], q.t[:, 2 * hd, :], True, False, [k.b, q.b], [pS.b])
            MM(S_, pS.t[:, hd * 128:(hd + 1) * 128], k.t[:, 2 * hd + 1, :], q.t[:, 2 * hd + 1, :], False, True,
               [k.b, q.b], [pS.b])
        pt = pts.next()
        TT(S_, "dve", pt.t[:], pS.t[:].rearrange("p (h n) -> p h n", h=4), cx.dmask.t[:], ALU.mult,
           [pS.b, cx.cb], [pt.b])
        for hd in range(4):
            vh = v.t[:, hd * 512:(hd + 1) * 512]
            pO = ps_o.next()
            MM(S_, pO.t[:], qd.t[:, 2 * hd, :], stb.t[:, hd, 0, :], True, False, [qd.b, stb.bs[2 * hd]], [pO.b])
            MM(S_, pO.t[:], qd.t[:, 2 * hd + 1, :], stb.t[:, hd, 1, :], False, False, [qd.b, stb.bs[2 * hd + 1]], [pO.b])
            MM(S_, pO.t[:], pt.t[:, hd, :], vh, False, True, [pt.b, v.b], [pO.b])
            ACT(S_, of.t[:, hd, :], pO.t[:], AF.Copy, [pO.b], [of.bs[hd], s1.b], accum_out=s1.t[:, hd:hd + 1])
            ACT(S_, junk.t[:], pO.t[:], AF.Square, [pO.b], [junk.b, s2.b], accum_out=s2.t[:, hd:hd + 1])
        if c + 1 < nchunks:
            for hd in range(4):
                vh = v.t[:, hd * 512:(hd + 1) * 512]
                for j in range(2):
                    pZ = ps_st.next()
                    MM(S_, pZ.t[:], kd.t[:, (2 * hd + j) * 128:(2 * hd + j + 1) * 128], vh, True, True,
                       [kd.b, v.b], [pZ.b])
                    bi = 2 * hd + j
                    STT(S_, "dve", stf.t[:, hd, j, :], stf.t[:, hd, j, :], cx.cdec.t[:, hd:hd + 1], pZ.t[:],
                        ALU.mult, ALU.add, [pZ.b, cx.cb], [stf.bs[bi]])
            for hd in range(4):
                for j in range(2):
                    bi = 2 * hd + j
                    ACT(S_, stb.t[:, hd, j, :], stf.t[:, hd, j, :], AF.Copy, [stf.bs[bi]], [stb.bs[bi]])
        nm, vr = nmean.next(), var.next()
        S_.op("dve", lambda e, o_=nm.t[:], i_=s1.t[:]: e.tensor_scalar_mul(out=o_, in0=i_, scalar1=-1.0 / 512),
              [s1.b], [nm.b])
        TT(S_, "dve", vr.t[:], nm.t[:], nm.t[:], ALU.mult, [nm.b], [vr.b])
        STT(S_, "dve", vr.t[:], s2.t[:], 1.0 / 512, vr.t[:], ALU.mult, ALU.subtract, [s2.b], [vr.b])
        ACT(S_, vr.t[:], vr.t[:], AF.Ln, [vr.b, cx.cb], [vr.b], bias=cx.epsc.t[:, 0:1], scale=1.0)
        ACT(S_, vr.t[:], vr.t[:], AF.Exp, [vr.b], [vr.b], scale=-0.5)
        yk = ytok.next()
        for hd in range(4):
            ty = tmpy.next()
            STT(S_, "dve", ty.t[:], of.t[:, hd, :], nm.t[:, hd:hd + 1], g.t[:, hd * 512:(hd + 1) * 512],
                ALU.add, ALU.mult, [of.bs[hd], nm.b, g.b], [ty.b])
            S_.op("dve", lambda e, o_=yk.t[:, hd * 512:(hd + 1) * 512], i_=ty.t[:], s_=vr.t[:, hd:hd + 1]:
                  e.tensor_scalar_mul(out=o_, in0=i_, scalar1=s_), [ty.b, vr.b], [yk.bs[hd]])
        def tail_fn(yk=yk, xt=xt, csl=csl):
            yt = yT.next()
            for half in range(2):
                for i in range(8):
                    ch = half * 8 + i
                    TR(S_, ps_tr.t[:, i * 128:(i + 1) * 128], yk.t[:, ch * 128:(ch + 1) * 128], cx.ident.t[:],
                       [yk.bs[ch // 4], cx.cb], [ps_tr.b])
                ACT(S_, yt.t[:, half * 8:(half + 1) * 8, :], ps_tr.t[:].rearrange("p (c t) -> p c t", c=8), AF.Copy,
                    [ps_tr.b], [yt.bs[half]])
            for half in range(2):
                pP = ps_op.next()
                for o4 in range(4):
                    oc = half * 4 + o4
                    for kc in range(16):
                        MM(S_, pP.t[:, o4 * 128:(o4 + 1) * 128], wo.t[:, oc, kc, :], yt.t[:, kc, :], kc == 0, kc == 15,
                           [wo.b, yt.bs[kc // 8]], [pP.b])
                TT(S_, "dve", xt.t[:, half * 4:(half + 1) * 4, :], pP.t[:].rearrange("p (c t) -> p c t", c=4),
                   xt.t[:, half * 4:(half + 1) * 4, :], ALU.add, [pP.b], [xt.b])
            S_.dma("pool", xout_v[:, :, csl], xt.t[:], reads=[xt.b])

        if pending_tail is not None:
            pending_tail()
        pending_tail = tail_fn
        if after_chunk is not None:
            after_chunk(c)
    if pending_tail is not None:
        pending_tail()


CF_LAYOUT = [("gains", 16), ("ffn_g", 16), ("ffn_cw", 264), ("qkg", 2), ("sconv", 12), ("kdec", 4),
             ("cdec", 4), ("gng", 16), ("dmask", 512), ("qdec", 2048),
             ("ones", 128), ("bdones", 128), ("onespad", 256), ("ident", 128), ("amask", 256), ("mbias", 256), ("amask2", 512), ("mbias2", 512)]
CF_OFF = {}
_o = 0
for _n, _w in CF_LAYOUT:
    CF_OFF[_n] = (_o, _w)
    _o += _w
NCF = _o


def build_program(Sx=S, phases="ABCDEFGH", out_name=None):
    from contextlib import ExitStack
    nc = bass.Bass("TRN2", target_bir_lowering=False)
    nt = Sx // NT

    def din(name, shape, dt=F32):
        return nc.dram_tensor(name, list(shape), dt, kind="ExternalInput").ap()

    def dint(name, shape, dt):
        return nc.dram_tensor(name, list(shape), dt, kind="Internal").ap()

    xT = din("xT", [D, Sx])
    cf = din("cf", [128, NCF])
    fw = {"l0_wfm": din("l0_wfm_f", [20, 128, 8, 128]), "l0_wv": din("l0_wv_f", [1, 128, 8, 512]),
          "l0_wo": din("l0_wo_f", [8, 128, 8, 128]),
          "ffn_up": din("ffn_up_f", [44, 128, 8, 256]), "ffn_dn": din("ffn_dn_f", [16, 128, 22, 128]),
          "r_wqk": din("r_wqk_f", [16, 128, 8, 128]), "r_wvg": din("r_wvg_f", [8, 128, 8, 512])}
    W = {"r_wo_f": din("r_wo_f", [8, 128, 16, 128]), "cos": din("cos", [128, Sx]), "sin": din("sin", [128, Sx])}
    yT = nc.dram_tensor("yT", [D, Sx], F32, kind="ExternalOutput").ap()
    bw = {k: dint(k + "_b", v.shape, BF16) for k, v in fw.items()}
    x1T = dint("x1T", [D, Sx], F32)
    x2T = dint("x2T", [D, Sx], F32)
    x3T = dint("x3T", [D, Sx], F32)
    W.update({"qT0": dint("qT0", [512, Sx], BF16), "kT0": dint("kT0", [512, Sx], BF16),
              "v0p": dint("v0p", [Sx, 1024], BF16), "abT": dint("abT", [D, Sx], BF16),
              "r_qT": dint("r_qT", [D, Sx], BF16), "r_qdT": dint("r_qdT", [D, Sx], BF16),
              "r_kT": dint("r_kT", [D, Sx], BF16), "r_kd": dint("r_kd", [Sx, D], BF16),
              "r_v": dint("r_v", [Sx, 2048], BF16), "r_g": dint("r_g", [Sx, 2048], F32),
              "r_wo": dint("r_wo", [8, 128, 16, 128], BF16)})
    cx = Ctx()
    cx.nc = nc
    cx.S = S_ = Sched(nc, {"sp": 28, "pool": 44})
    cx.sb = A = SbAlloc(nc)
    cx.C = None
    cx.consts_b = cx.cb = Buf(const=True)
    cfs = TB(A.alloc([128, NCF], F32, persist=True))
    S_.dma("sp", cfs.t[:], cf, writes=[cx.cb])

    class V:
        pass

    def fview(name, shape):
        o, w = CF_OFF[name]
        v = V()
        ap = cfs.t[:, o:o + w]
        if len(shape) == 2:
            v.t = ap.rearrange("p (a b) -> p a b", a=shape[0])
        elif len(shape) == 3:
            v.t = ap.rearrange("p (a b c) -> p a b c", a=shape[0], b=shape[1])
        else:
            v.t = ap
        v.b = cx.cb
        return v

    cx.gains = fview("gains", (2, 8))
    cx.ffn_g = fview("ffn_g", (2, 8))
    cx.ffn_cw = fview("ffn_cw", (2, 44, 3))
    cx.qkg = fview("qkg", ())
    cx.sconv = fview("sconv", (4, 3))
    cx.kdec = fview("kdec", ())
    cx.cdec = fview("cdec", ())
    cx.gng = fview("gng", ())
    cx.dmask = fview("dmask", (4, 128))
    cx.qdec = fview("qdec", (4, 512))
    cx.epsc = TB(A.alloc([128, 1], F32, persist=True))
    cx.epsc.b.const = True
    MEMSET(S_, "pool", cx.epsc.t[:], EPS, [cx.epsc.b])

    def bview(name, shape):
        o, w = CF_OFF[name]
        t = TB(A.alloc([128] + list(shape), BF16, persist=True))
        t.b = cx.cb
        src = cfs.t[:, o:o + w]
        if len(shape) == 2:
            src = src.rearrange("p (a b) -> p a b", a=shape[0])
        S_.op("dve", lambda e, o_=t.t[:], i_=src: e.tensor_copy(out=o_, in_=i_), [cx.cb], [Buf()])
        return t

    cx.ones = bview("ones", (128,))
    cx.bdones = bview("bdones", (128,))
    cx.onespad = bview("onespad", (2, 128))
    cx.ident = bview("ident", (128,))
    cx.amask = bview("amask", (256,))
    cx.mbias = bview("mbias", (256,))
    cx.amask2 = bview("amask2", (2, 256))
    cx.mbias2 = bview("mbias2", (2, 256))
    cx.psum = [TB(nc.alloc_psum_tensor("ps%d" % i, [128, 512], F32)) for i in range(8)]
    cx.psT = TB(cx.psum[5].t.bitcast(BF16))
    S_.barrier()
    bufs = {}

    def do_casts(keys):
        for k, lo, hi in keys:
            step = 4 if k in ("l0_wfm", "ffn_up", "r_wqk") else (2 if k in ("ffn_dn", "l0_wo", "r_wvg") else 1)
            bl = bufs.setdefault(k, [None] * fw[k].shape[0])
            for i in range(lo, hi, step):
                b = Buf(const=True)
                e_ = min(hi, i + step)
                S_.dma("pool", bw[k][i:e_], fw[k][i:e_], writes=[b], track=False)
                for j in range(i, e_):
                    bl[j] = b

    do_casts([("l0_wfm", 0, 20), ("l0_wv", 0, 1)])

    def spread(ti, keys, nparts=4, fine=False):
        if ti >= nparts:
            return
        for k, lo, hi in keys:
            step = 4 if k in ("l0_wfm", "ffn_up", "r_wqk") else (2 if k in ("ffn_dn", "l0_wo", "r_wvg") else 1)
            if fine:
                step = 1
            n = hi - lo
            per = -(-n // nparts)
            per = -(-per // step) * step
            a0 = lo + ti * per
            a1 = min(hi, a0 + per)
            if a0 < a1:
                do_casts([(k, a0, a1)])

    def ensure(keys):
        for k, lo, hi in keys:
            if k not in bufs:
                do_casts([(k, lo, hi)])
            else:
                for j in range(lo, hi):
                    if bufs[k][j] is None:
                        do_casts([(k, j, j + 1)])
    W.update({"l0_wfm": bw["l0_wfm"], "l0_wv": bw["l0_wv"][0], "l0_wo": bw["l0_wo"],
              "r_wqk": bw["r_wqk"], "r_wvg": bw["r_wvg"]})

    class AllBufs(Buf):
        pass

    def allb(k):
        b = Buf(const=True)
        b.w = [o for x in bufs[k] for o in x.w]
        return b

    class Lazy:
        const = True
        r = []

        def __init__(self, fn):
            self.fn = fn

        @property
        def w(self):
            return self.fn()

    W["b_l0"] = allb("l0_wv")
    W["bl_l0_wfm"] = bufs["l0_wfm"]
    W["b_l0o"] = Lazy(lambda: allb("l0_wo").w)
    W["b_rqk"] = Lazy(lambda: allb("r_wqk").w)
    W["b_rvg"] = Lazy(lambda: allb("r_wvg").w)
    W["b_rwo"] = Buf(const=True)
    cur = xT
    nxt = {"C": x1T, "D": x2T, "G": x3T, "H": yT}
    if "A" in phases:
        emit_l0_inproj(cx, xT, W, nt, after_tile0=lambda ti: spread(ti, [("l0_wo", 0, 8)]))
        S_.barrier()
    if "B" in phases:
        emit_l0_attn(cx, W, Sx, after_pair0=None, wo_staged=("E" in phases),
                     after_pair=lambda c: spread(c, [("ffn_up", 0, 22), ("ffn_dn", 0, 8)], nparts=3))
        S_.barrier()
    fuse_c = ("C" in phases) and ("D" in phases)
    if "C" in phases and not fuse_c:
        ensure([("l0_wo", 0, 8)])
        dst = yT if out_name == "C" else x1T
        emit_outproj(cx, cur, dst, W["abT"], W["l0_wo"], W["b_l0o"], 8, nt)
        cur = dst
        S_.barrier()
    if "D" in phases:
        ensure([("ffn_up", 0, 22), ("ffn_dn", 0, 8)])
        dst = yT if out_name == "D" else x2T
        b = Buf(const=True)
        b.w = [o for x in bufs["ffn_up"][0:22] + bufs["ffn_dn"][0:8] for o in x.w]
        if fuse_c:
            ensure([("l0_wo", 0, 8)])
        emit_ffn(cx, cur, dst, bw["ffn_up"][0:22], bw["ffn_dn"][0:8], 0, nt, wbufs=[b],
                 after_tile0=lambda ti: spread(ti, [("r_wqk", 0, 16), ("r_wvg", 0, 8)], nparts=7, fine=True),
                 pre=({"actT": W["abT"], "wo": W["l0_wo"], "wbuf": W["b_l0o"]} if fuse_c else None))
        cur = dst
        S_.barrier()
    if "E" in phases and "B" not in phases:
        emit_wo_scale(cx, W)
        S_.barrier()
    if "F" in phases:
        ensure([("r_wqk", 0, 16), ("r_wvg", 0, 8)])
        emit_ret_proj(cx, cur, W, nt)
        S_.barrier()
    if "G" in phases:
        dst = yT if out_name == "G" else x3T
        emit_ret_core(cx, cur, dst, W, Sx // 128,
                      after_chunk=lambda c: spread(c // 2, [("ffn_up", 22, 44), ("ffn_dn", 8, 16)], nparts=12, fine=True)
                      if c % 2 == 0 else None)
        cur = dst
        S_.barrier()
    if "H" in phases:
        ensure([("ffn_up", 22, 44), ("ffn_dn", 8, 16)])
        b = Buf(const=True)
        b.w = [o for x in bufs["ffn_up"][22:44] + bufs["ffn_dn"][8:16] for o in x.w]
        emit_ffn(cx, cur, yT, bw["ffn_up"][22:44], bw["ffn_dn"][8:16], 1, nt, wbufs=[b])
        S_.barrier()
    cx.S.barrier()
    with ExitStack() as stack:
        S_.emit(stack)
    return nc


def arr_fm(Wm):
    K, N = Wm.shape
    return np.ascontiguousarray(Wm.reshape(K // 128, 128, N // 128, 128).transpose(2, 1, 0, 3))


def arr_tm(Wm, slab=512):
    K, N = Wm.shape
    return np.ascontiguousarray(Wm.reshape(K // 128, 128, N // slab, slab).transpose(2, 1, 0, 3))


def host_consts(inp, Sx):
    f = np.float32
    cfd = {}
    g = np.zeros((128, 2, 8), f)
    g[:, 0, :] = inp["even_norm"][0].reshape(8, 128).T
    g[:, 1, :] = inp["odd_norm"][0].reshape(8, 128).T
    cfd["gains"] = g
    cfd["ffn_g"] = np.stack([inp["ffn_norm"][l].reshape(8, 128).T for l in range(2)], axis=1)
    cfd["ffn_cw"] = np.stack([inp["ffn_conv_w"][l].reshape(3, 44, 128).transpose(2, 1, 0) for l in range(2)], axis=1)
    cfd["qkg"] = np.stack([np.tile(inp["even_q_gain"][0], 2), np.tile(inp["even_k_gain"][0], 2)], axis=1)
    cfd["sconv"] = inp["even_sconv_w"][0].reshape(3, 4, 128).transpose(2, 1, 0)
    H = 4
    log_g = np.log1p(-(2.0 ** (-5.0 - np.arange(H, dtype=np.float64))))
    i = np.arange(128, dtype=np.float64)
    cfd["kdec"] = np.exp(log_g[None, :] * (127.0 - i[:, None]))
    cfd["cdec"] = np.tile(np.exp(log_g * 128.0)[None, :], (128, 1))
    cfd["gng"] = inp["ret_gn_gain"][0].reshape(16, 128).T
    diff = i[None, :] - i[:, None]
    dm = np.where(diff[None] >= 0, np.exp(log_g[:, None, None] * np.maximum(diff[None], 0.0)), 0.0)
    cfd["dmask"] = dm.transpose(1, 0, 2)
    qd = np.exp(log_g[:, None] * (i[None, :] + 1.0))
    cfd["qdec"] = np.tile(np.tile(qd, (1, 4))[None], (128, 1, 1))
    cfd["ones"] = np.ones((128, 128))
    bd = np.zeros((128, 128))
    bd[:64, :64] = 1
    bd[64:, 64:] = 1
    cfd["bdones"] = bd
    op = np.zeros((128, 2, 128))
    op[:, 0, :64] = 1
    op[:, 1, 64:] = 1
    cfd["onespad"] = op
    cfd["ident"] = np.eye(128)
    am = np.zeros((128, 256))
    j = np.arange(128)
    am[:, :128] = (j[None, :] >= j[:, None])
    am[:, 128:] = (j[None, :] <= j[:, None])
    cfd["amask"] = am
    cfd["mbias"] = (am - 1.0) * 30000.0
    cfd["amask2"] = np.stack([am, am], axis=1)
    cfd["mbias2"] = (cfd["amask2"] - 1.0) * 30000.0
    cf = np.zeros((128, NCF), f)
    for n, (o, w) in CF_OFF.items():
        cf[:, o:o + w] = np.asarray(cfd[n], dtype=np.float64).reshape(128, w)
    half = 128
    inv = 10000.0 ** (-np.arange(half, dtype=np.float64) / half)
    ang = inv[:, None] * np.arange(Sx, dtype=np.float64)[None, :]
    ang32 = (np.arange(Sx, dtype=f)[None, :] * (10000.0 ** (-np.arange(half, dtype=f) / half)).astype(f)[:, None]).astype(f)
    return cf, np.cos(ang32).astype(f), np.sin(ang32).astype(f)


def host_weights(inp):
    w_in = inp["even_w_in"][0]
    fm_cols = np.concatenate([w_in[:, 0:1024], w_in[:, 1536:3072]], axis=1)
    up = np.stack([inp["ffn_w_up"][l].reshape(8, 128, 2, 22, 128).transpose(3, 1, 0, 2, 4).reshape(22, 128, 8, 256)
                   for l in range(2)]).reshape(44, 128, 8, 256)
    dn = np.stack([arr_fm(inp["ffn_w_down"][l]) for l in range(2)]).reshape(16, 128, 22, 128)
    return {
        "l0_wfm_f": arr_fm(fm_cols),
        "l0_wv_f": arr_tm(w_in[:, 1024:1536]),
        "l0_wo_f": arr_fm(inp["even_w_out"][0]),
        "ffn_up_f": np.ascontiguousarray(up),
        "ffn_dn_f": np.ascontiguousarray(dn),
        "r_wqk_f": np.concatenate([arr_fm(inp["ret_wq"][0]), arr_fm(inp["ret_wk"][0])], axis=0),
        "r_wvg_f": np.concatenate([arr_tm(inp["ret_wv"][0]), arr_tm(inp["ret_wg"][0])], axis=0),
        "r_wo_f": arr_fm(inp["ret_wo"][0]),
    }


_CACHE = {}


def kernel(**inputs):
    inp = {k: np.asarray(v, dtype=np.float32) for k, v in inputs.items()}
    x = inp["x"]
    B = x.shape[0]
    if "nc" not in _CACHE:
        _CACHE["nc"] = build_program(S)
    nc = _CACHE["nc"]
    cf, cos, sin = host_consts(inp, S)
    hw = host_weights(inp)
    in_maps = []
    for b in range(B):
        m = {"xT": np.ascontiguousarray(x[b].T), "cf": cf, "cos": cos, "sin": sin}
        m.update(hw)
        in_maps.append(m)
    res = run_bass_kernel_spmd(nc, in_maps, core_ids=list(range(B)))
    out = np.stack([np.ascontiguousarray(res.results[b]["yT"].T) for b in range(B)], axis=0)
    return out.astype(np.float32)
```

```python
import numpy as np
import concourse.bass as bass
import concourse.mybir as mybir
from concourse.bass_utils import run_bass_kernel_spmd

F32 = mybir.dt.float32
BF16 = mybir.dt.bfloat16
AF = mybir.ActivationFunctionType
ALU = mybir.AluOpType
AX = mybir.AxisListType

D = 1024
S = 4096
NT = 512
DFF = 2816
EPS = 1e-6
SB_BASE = 18432
SB_END = 229376


class Buf:
    __slots__ = ("w", "r", "const")

    def __init__(self, const=False):
        self.w = []
        self.r = []
        self.const = const


class Op:
    __slots__ = ("eng", "fn", "deps", "signal", "dsem", "dval", "val", "idx", "dprev")

    def __init__(self, eng, fn, deps):
        self.eng = eng
        self.fn = fn
        self.deps = deps
        self.signal = False
        self.dsem = None
        self.dval = 0
        self.val = 0


class DSem:
    def __init__(self):
        self.count = 0
        self.sem = None


ENGS = ("pe", "act", "dve", "pool", "sp")


class Sched:
    def __init__(self, nc, n_dma_sems):
        self.nc = nc
        self.q = {e: [] for e in ENGS}
        self.dpool = {e: [DSem() for _ in range(n)] for e, n in n_dma_sems.items()}
        self.dnext = {e: 0 for e in n_dma_sems}
        self.all_dma = []

    def _mk(self, eng, fn, reads, writes, deps, is_dma=False):
        d = list(deps)
        for b in reads:
            d.extend(b.w)
        for b in writes:
            d.extend(b.w)
            d.extend(b.r)
        o = Op(eng, fn, d)
        for b in reads:
            if not b.const:
                if not is_dma:
                    b.r = [x for x in b.r if not (x.eng == eng and x.dsem is None)]
                b.r.append(o)
        for b in writes:
            b.w = [o]
            b.r = []
        for x in d:
            if x.dsem is None and not (x.eng == "pe" and eng == "pe"):
                x.signal = True
        o.idx = len(self.q[eng])
        self.q[eng].append(o)
        return o

    def op(self, eng, fn, reads=(), writes=(), deps=()):
        return self._mk(eng, fn, reads, writes, deps)

    def dma(self, eng, out, in_, reads=(), writes=(), deps=(), track=True):
        pool = self.dpool[eng]
        ds = pool[self.dnext[eng] % len(pool)]
        self.dnext[eng] += 1
        o = self._mk(eng, lambda e: e.dma_start(out=out, in_=in_), reads, writes, deps, is_dma=True)
        o.dsem = ds
        o.dval = ds.count + 16
        o.dprev = ds.count
        ds.count += 16
        if track:
            self.all_dma.append(o)
        return o

    def barrier(self):
        lasts = []
        for e in ENGS:
            for o in reversed(self.q[e]):
                if o.dsem is None and o.fn is not None:
                    lasts.append(o)
                    break
        dm = list(self.all_dma)
        self.all_dma = []
        for e in ENGS:
            self._mk(e, None, (), (), lasts + dm)

    def emit(self, stack):
        nc = self.nc
        esem = {}
        for e in ("pe", "act", "dve", "pool"):
            esem[e] = stack.enter_context(nc.semaphore("prog_" + e))
        for e, pool in self.dpool.items():
            for i, ds in enumerate(pool):
                ds.sem = stack.enter_context(nc.semaphore("dma_%s_%d" % (e, i)))
        for e in ENGS:
            run = 0
            for o in self.q[e]:
                if o.dsem is None and o.signal:
                    run += 1
                    o.val = run
        self.nsig = {e: sum(1 for o in self.q[e] if o.signal and o.dsem is None) for e in ENGS}
        for sm in list(esem.values()) + [ds.sem for pool in self.dpool.values() for ds in pool]:
            nc.gpsimd.sem_clear(sm)
        nc.all_engine_barrier()
        block = stack.enter_context(nc.Block())
        q = self.q

        def run_engine(ename, eng):
            waited = {}

            def wait(sem, val):
                if val <= 0:
                    return
                k = sem.name
                if waited.get(k, 0) >= val:
                    return
                eng.wait_ge(sem, val)
                waited[k] = val

            for o in q[ename]:
                for d in o.deps:
                    if d.dsem is not None:
                        wait(d.dsem.sem, d.dval)
                    else:
                        if d.eng == "pe" and ename == "pe":
                            continue
                        wait(esem[d.eng], d.val)
                if o.dsem is not None:
                    wait(o.dsem.sem, o.dprev)
                if o.fn is None:
                    continue
                ins = o.fn(eng)
                if o.dsem is not None:
                    ins.then_inc(o.dsem.sem, 16)
                elif o.signal:
                    ins.then_inc(esem[ename], 1)

        @block.tensor
        def _(e):
            run_engine("pe", e)

        @block.scalar
        def _(e):
            run_engine("act", e)

        @block.vector
        def _(e):
            run_engine("dve", e)

        @block.gpsimd
        def _(e):
            run_engine("pool", e)

        @block.sync
        def _(e):
            run_engine("sp", e)


class SbAlloc:
    def __init__(self, nc):
        self.nc = nc
        self.persist = SB_BASE
        self.cur = SB_BASE
        self.n = 0

    def alloc(self, shape, dtype, persist=False):
        nbytes = int(np.prod(shape[1:])) * (4 if dtype == F32 else 2)
        nbytes = (nbytes + 63) // 64 * 64
        self.n += 1
        if persist:
            assert self.cur == self.persist, "persistent allocs must come first"
            off = self.persist
            self.persist += nbytes
            self.cur = self.persist
        else:
            off = self.cur
            self.cur += nbytes
        assert self.cur <= SB_END, "SBUF overflow: %d" % self.cur
        return self.nc.alloc_sbuf_tensor_at("sb%d" % self.n, list(shape), dtype, offset=off)

    def reset(self):
        self.cur = self.persist


class Ring:
    def __init__(self, items):
        self.items = items
        self.i = 0

    def next(self):
        x = self.items[self.i % len(self.items)]
        self.i += 1
        return x


class TB:
    def __init__(self, t, nb=1):
        self.t = t
        self.bs = [Buf() for _ in range(nb)]
        self.b = self.bs[0]


class Ctx:
    pass


def emit_rmsnorm_tile(cx, xt, gain_ap_fn, xn, nrm_div, ps_ss):
    S_ = cx.S
    sq = cx.sq
    S_.op("act", lambda e: e.activation(out=sq.t[:], in_=xt.t[:], func=AF.Square),
          reads=xt.bs, writes=[sq.b])
    for c in range(8):
        S_.op("pe", lambda e, c=c: e.matmul(ps_ss.t[:], cx.ones.t[:], sq.t[:, c, :],
                                             start=(c == 0), stop=(c == 7)),
              reads=[sq.b, cx.ones.b], writes=[ps_ss.b])
    rs = cx.rstd.next()
    S_.op("act", lambda e: e.activation(out=rs.t[:], in_=ps_ss.t[:], func=AF.Ln,
                                        bias=cx.epsc.t[:, 0:1], scale=1.0 / nrm_div),
          reads=[ps_ss.b, cx.epsc.b], writes=[rs.b])
    S_.op("act", lambda e: e.activation(out=rs.t[:], in_=rs.t[:], func=AF.Exp, scale=-0.5),
          reads=[rs.b], writes=[rs.b])
    for c in range(8):
        S_.op("dve", lambda e, c=c: e.scalar_tensor_tensor(
            out=xn.t[:, c, :], in0=xt.t[:, c, :], scalar=gain_ap_fn(c), in1=rs.t[:],
            op0=ALU.mult, op1=ALU.mult),
            reads=[xt.bs[c], rs.b, cx.consts_b], writes=[xn.bs[c]])


def emit_ffn(cx, xin, xout, wup, wdn, l, ntiles, wbufs=(), after_tile0=None, pre=None):
    S_ = cx.S
    A = cx.sb
    A.reset()
    xin_v = xin.rearrange("(c p) s -> p c s", p=128)
    xout_v = xout.rearrange("(c p) s -> p c s", p=128)

    xts = Ring([TB(A.alloc([128, 8, NT], F32), nb=8) for _ in range(2)])
    cx.sq = TB(A.alloc([128, 8, NT], BF16))
    cx.rstd = Ring([TB(A.alloc([128, NT], F32)) for _ in range(1)])
    xns = Ring([TB(A.alloc([128, 8, NT], BF16), nb=8) for _ in range(2)])
    wups = Ring([TB(A.alloc([128, 8, 256], BF16)) for _ in range(3)])
    wdns = Ring([TB(A.alloc([128, 22, 128], BF16)) for _ in range(3)])
    ubs = Ring([TB(A.alloc([128, 2, NT + 2], F32), nb=2) for _ in range(2)])
    ys = Ring([TB(A.alloc([128, 2, NT], F32), nb=2) for _ in range(3)])
    sgs = Ring([TB(A.alloc([128, NT], F32)) for _ in range(2)])
    hs = Ring([TB(A.alloc([128, 22, NT], BF16), nb=22) for _ in range(2)])
    carry = TB(A.alloc([128, 22, 2, 2], F32), nb=22)
    pups = Ring([(cx.psum[0], cx.psum[1]), (cx.psum[2], cx.psum[3])])
    pos = Ring([cx.psum[4], cx.psum[5]])
    ps_ss = cx.psum[6]

    S_.op("pool", lambda e: e.memset(carry.t[:], 0.0), writes=carry.bs)
    if pre is not None:
        act_v = pre["actT"].rearrange("(c p) s -> p c s", p=128)
        ats = Ring([TB(A.alloc([128, 8, NT], BF16)) for _ in range(2)])
        wos = Ring([TB(A.alloc([128, 8, 128], BF16)) for _ in range(2)])

    cw = cx.ffn_cw.t

    def down(xt, h, ti, ocs=range(8), store=True):
        for oc in ocs:
            wd = wdns.next()
            S_.dma("sp", wd.t[:], wdn[oc], reads=list(wbufs), writes=[wd.b])
            po = pos.next()
            for kc in range(22):
                S_.op("pe", lambda e, kc=kc, wd=wd, po=po: e.matmul(
                    po.t[:], wd.t[:, kc, :], h.t[:, kc, :], start=(kc == 0), stop=(kc == 21)),
                    reads=[wd.b, h.bs[kc]], writes=[po.b])
            S_.op("dve", lambda e, oc=oc, po=po: e.tensor_tensor(
                out=xt.t[:, oc, :], in0=po.t[:], in1=xt.t[:, oc, :], op=ALU.add),
                reads=[po.b], writes=[xt.bs[oc]])
        if store:
            S_.dma("pool", xout_v[:, :, ti * NT:(ti + 1) * NT], xt.t[:], reads=xt.bs)

    def gate(y, j, h):
        sg = sgs.next()
        S_.op("act", lambda e, y=y, sg=sg: e.activation(out=sg.t[:], in_=y.t[:, 0, :], func=AF.Silu),
              reads=[y.bs[0]], writes=[sg.b])
        S_.op("dve", lambda e, y=y, sg=sg, j=j, h=h: e.tensor_tensor(
            out=h.t[:, j, :], in0=sg.t[:], in1=y.t[:, 1, :], op=ALU.mult),
            reads=[sg.b, y.bs[1]], writes=[h.bs[j]])

    pend_gate = None
    prev = None
    for ti in range(ntiles):
        xt = xts.next()
        S_.dma("sp", xt.t[:], xin_v[:, :, ti * NT:(ti + 1) * NT], writes=xt.bs)
        if pre is not None:
            at = ats.next()
            S_.dma("sp", at.t[:], act_v[:, :, ti * NT:(ti + 1) * NT], writes=[at.b])
            for oc in range(8):
                wo_ = wos.next()
                S_.dma("sp", wo_.t[:], pre["wo"][oc], reads=[pre["wbuf"]], writes=[wo_.b])
                po = pos.next()
                for kc in range(8):
                    S_.op("pe", lambda e, kc=kc, wo_=wo_, po=po, at=at: e.matmul(
                        po.t[:], wo_.t[:, kc, :], at.t[:, kc, :], start=(kc == 0), stop=(kc == 7)),
                        reads=[wo_.b, at.b], writes=[po.b])
                S_.op("dve", lambda e, oc=oc, po=po, xt=xt: e.tensor_tensor(
                    out=xt.t[:, oc, :], in0=po.t[:], in1=xt.t[:, oc, :], op=ALU.add),
                    reads=[po.b], writes=[xt.bs[oc]])
            if prev is not None:
                down(*prev, ocs=range(0, 4), store=False)
        xn = xns.next()
        emit_rmsnorm_tile(cx, xt, lambda c: cx.ffn_g.t[:, l, c:c + 1], xn, float(D), ps_ss)
        if prev is not None:
            if pre is not None:
                down(*prev, ocs=range(4, 8), store=True)
            else:
                down(*prev)
        h = hs.next()
        for j in range(22):
            wu = wups.next()
            S_.dma("sp", wu.t[:], wup[j], reads=list(wbufs), writes=[wu.b])
            pg, pv = pups.next()
            for half, pp in ((0, pg), (1, pv)):
                for kc in range(8):
                    S_.op("pe", lambda e, kc=kc, half=half, pp=pp, wu=wu, xn=xn: e.matmul(
                        pp.t[:], wu.t[:, kc, half * 128:(half + 1) * 128], xn.t[:, kc, :],
                        start=(kc == 0), stop=(kc == 7)),
                        reads=[wu.b, xn.bs[kc]], writes=[pp.b])
            ub = ubs.next()
            y = ys.next()
            S_.op("act", lambda e, ub=ub, j=j: e.activation(
                out=ub.t[:, :, 0:2], in_=carry.t[:, j, :, :], func=AF.Copy),
                reads=[carry.bs[j]], writes=ub.bs)
            for half, pp in ((0, pg), (1, pv)):
                ch = j + 22 * half
                S_.op("act", lambda e, half=half, pp=pp, ub=ub: e.activation(
                    out=ub.t[:, half, 2:NT + 2], in_=pp.t[:], func=AF.Copy),
                    reads=[pp.b], writes=[ub.bs[half]])
                S_.op("act", lambda e, half=half, pp=pp, y=y, ch=ch: e.activation(
                    out=y.t[:, half, :], in_=pp.t[:], func=AF.Identity, scale=cw[:, l, ch, 2:3]),
                    reads=[pp.b, cx.consts_b], writes=[y.bs[half]])
            S_.op("act", lambda e, ub=ub, j=j: e.activation(
                out=carry.t[:, j, :, :], in_=ub.t[:, :, NT:NT + 2], func=AF.Copy),
                reads=ub.bs, writes=[carry.bs[j]])
            for k in (1, 0):
                for half in (0, 1):
                    ch = j + 22 * half
                    S_.op("dve", lambda e, half=half, ub=ub, y=y, ch=ch, k=k: e.scalar_tensor_tensor(
                        out=y.t[:, half, :], in0=ub.t[:, half, k:k + NT], scalar=cw[:, l, ch, k:k + 1],
                        in1=y.t[:, half, :], op0=ALU.mult, op1=ALU.add),
                        reads=[ub.bs[half], cx.consts_b], writes=[y.bs[half]])
            if pend_gate is not None:
                gate(*pend_gate)
            pend_gate = (y, j, h)
        gate(*pend_gate)
        pend_gate = None
        prev = (xt, h, ti)
        if after_tile0 is not None:
            after_tile0(ti)
    down(*prev)


def MM(S_, out, lhsT, rhs, start, stop, reads, writes):
    return S_.op("pe", lambda e: e.matmul(out, lhsT, rhs, start=start, stop=stop), reads, writes)


def TR(S_, out, in_, ident, reads, writes):
    return S_.op("pe", lambda e: e.transpose(out, in_, ident), reads, writes)


def ACT(S_, out, in_, func, reads, writes, **kw):
    return S_.op("act", lambda e: e.activation(out=out, in_=in_, func=func, **kw), reads, writes)


def TT(S_, eng, out, in0, in1, op, reads, writes):
    return S_.op(eng, lambda e: e.tensor_tensor(out=out, in0=in0, in1=in1, op=op), reads, writes)


def STT(S_, eng, out, in0, scalar, in1, op0, op1, reads, writes):
    return S_.op(eng, lambda e: e.scalar_tensor_tensor(out=out, in0=in0, scalar=scalar, in1=in1,
                                                       op0=op0, op1=op1), reads, writes)


def TS(S_, eng, out, in0, s1, s2, op0, op1, reads, writes):
    return S_.op(eng, lambda e: e.tensor_scalar(out=out, in0=in0, scalar1=s1, scalar2=s2,
                                                op0=op0, op1=op1), reads, writes)


def CP(S_, eng, out, in_, reads, writes):
    return S_.op(eng, lambda e: e.tensor_copy(out=out, in_=in_), reads, writes)


def MEMSET(S_, eng, out, val, writes):
    return S_.op(eng, lambda e: e.memset(out, val), (), writes)


def emit_l0_inproj(cx, xin, W, ntiles, after_tile0=None):
    S_ = cx.S
    A = cx.sb
    A.reset()
    C = cx.C
    xin_v = xin.rearrange("(c p) s -> p c s", p=128)
    xts = Ring([TB(A.alloc([128, 8, NT], F32), nb=8) for _ in range(2)])
    cx.sq = TB(A.alloc([128, 8, NT], BF16))
    cx.rstd = Ring([TB(A.alloc([128, NT], F32)) for _ in range(2)])
    xns = Ring([TB(A.alloc([128, 8, NT], BF16), nb=8) for _ in range(2)])
    wr = Ring([TB(A.alloc([128, 8, 128], BF16)) for _ in range(8)])
    wv = TB(A.alloc([128, 8, 512], BF16))
    sqh = Ring([TB(A.alloc([128, NT], BF16)) for _ in range(2)])
    rsh = Ring([TB(A.alloc([128, NT], F32)) for _ in range(2)])
    ost = Ring([TB(A.alloc([128, NT], BF16)) for _ in range(3)])
    vst = Ring([TB(A.alloc([128, 4, 2, 128], BF16)) for _ in range(3)])
    for vs_ in vst.items:
        MEMSET(S_, "pool", vs_.t[:], 0.0, [vs_.b])
    tmpf = Ring([TB(A.alloc([128, NT], F32)) for _ in range(2)])
    chs = Ring([TB(A.alloc([128, NT + 2], F32)) for _ in range(2)])
    ybs = Ring([TB(A.alloc([128, NT], F32)) for _ in range(2)])
    carry = TB(A.alloc([128, 4, 2], F32), nb=4)
    pr = Ring([cx.psum[0], cx.psum[1], cx.psum[2], cx.psum[3], cx.psum[5], cx.psum[7]])
    ps_h = cx.psum[4]
    ps_ss = cx.psum[6]
    MEMSET(S_, "pool", carry.t[:], 0.0, carry.bs)
    S_.dma("sp", wv.t[:], W["l0_wv"], reads=[W["b_l0"]], writes=[wv.b])
    sw = cx.sconv.t

    def proj(oc, xn):
        w = wr.next()
        S_.dma("sp", w.t[:], W["l0_wfm"][oc], reads=[W["bl_l0_wfm"][oc]], writes=[w.b])
        p = pr.next()
        for kc in range(8):
            MM(S_, p.t[:], w.t[:, kc, :], xn.t[:, kc, :], kc == 0, kc == 7, [w.b, xn.bs[kc]], [p.b])
        return p

    for ti in range(ntiles):
        tsl = slice(ti * NT, (ti + 1) * NT)
        xt = xts.next()
        S_.dma("sp", xt.t[:], xin_v[:, :, tsl], writes=xt.bs)
        xn = xns.next()
        emit_rmsnorm_tile(cx, xt, lambda c: cx.gains.t[:, 0, c:c + 1], xn, float(D), ps_ss)
        def finish(oc, p, sq):
            MM(S_, ps_h.t[:], cx.bdones.t[:], sq.t[:], True, True, [sq.b, cx.cb], [ps_h.b])
            rs = rsh.next()
            ACT(S_, rs.t[:], ps_h.t[:], AF.Ln, [ps_h.b, cx.cb], [rs.b], bias=cx.epsc.t[:, 0:1], scale=1.0 / 64)
            ACT(S_, rs.t[:], rs.t[:], AF.Exp, [rs.b], [rs.b], scale=-0.5)
            o = ost.next()
            gi = 0 if oc < 4 else 1
            STT(S_, "dve", o.t[:], p.t[:], cx.qkg.t[:, gi:gi + 1], rs.t[:], ALU.mult, ALU.mult,
                [p.b, rs.b, cx.cb], [o.b])
            dst = W["qT0"] if oc < 4 else W["kT0"]
            r0 = (oc % 4) * 128
            S_.dma("pool", dst[r0:r0 + 128, tsl], o.t[:], reads=[o.b])

        pend = None
        for oc in range(8):
            p = proj(oc, xn)
            sq = sqh.next()
            ACT(S_, sq.t[:], p.t[:], AF.Square, [p.b], [sq.b])
            if pend is not None:
                finish(*pend)
            pend = (oc, p, sq)
        finish(*pend)
        for sub in range(4):
            p = pr.next()
            for kc in range(8):
                MM(S_, p.t[:], xn.t[:, kc, sub * 128:(sub + 1) * 128], wv.t[:, kc, :], kc == 0, kc == 7,
                   [wv.b, xn.bs[kc]], [p.b])
            o = vst.next()
            pv4 = p.t[:].rearrange("p (c h e) -> p c h e", c=4, h=2)
            for hh in range(2):
                ACT(S_, o.t[:, :, hh, hh * 64:(hh + 1) * 64], pv4[:, :, hh, :], AF.Copy, [p.b], [o.b])
            r0 = ti * NT + sub * 128
            S_.dma("pool", W["v0p"][r0:r0 + 128, :], o.t[:].rearrange("p c h e -> p (c h e)"), reads=[o.b])
        for i in range(4):
            pgb = proj(8 + i, xn)
            pgc = proj(12 + i, xn)
            ph = proj(16 + i, xn)
            tf = tmpf.next()
            ACT(S_, tf.t[:], pgc.t[:], AF.Copy, [pgc.b], [tf.b])
            ch = chs.next()
            ACT(S_, ch.t[:, 0:2], carry.t[:, i, :], AF.Copy, [carry.bs[i]], [ch.b])
            TT(S_, "dve", ch.t[:, 2:NT + 2], tf.t[:], ph.t[:], ALU.mult, [tf.b, ph.b], [ch.b])
            ACT(S_, carry.t[:, i, :], ch.t[:, NT:NT + 2], AF.Copy, [ch.b], [carry.bs[i]])
            yb = ybs.next()
            ACT(S_, yb.t[:], ch.t[:, 2:NT + 2], AF.Identity, [ch.b, cx.cb], [yb.b], scale=sw[:, i, 2:3])
            STT(S_, "dve", yb.t[:], ch.t[:, 1:NT + 1], sw[:, i, 1:2], yb.t[:], ALU.mult, ALU.add,
                [ch.b, cx.cb], [yb.b])
            STT(S_, "dve", yb.t[:], ch.t[:, 0:NT], sw[:, i, 0:1], yb.t[:], ALU.mult, ALU.add,
                [ch.b, cx.cb], [yb.b])
            o = ost.next()
            TT(S_, "dve", o.t[:], yb.t[:], pgb.t[:], ALU.mult, [yb.b, pgb.b], [o.b])
            r0 = 512 + i * 128
            S_.dma("pool", W["abT"][r0:r0 + 128, tsl], o.t[:], reads=[o.b])
        if after_tile0 is not None:
            after_tile0(ti)


def emit_l0_attn(cx, W, Sx, after_pair0=None, after_pair=None, wo_staged=False):
    S_ = cx.S
    A = cx.sb
    A.reset()
    qpad = TB(A.alloc([128, 2, Sx], BF16))
    kT = TB(A.alloc([128, Sx], BF16))
    acc = TB(A.alloc([128, 2, Sx], F32))
    NBMAX = Sx // 128
    vbs = Ring([TB(A.alloc([128, NBMAX, 2, 128], BF16)) for _ in range(3)])
    pts = Ring([TB(A.alloc([128, 2, 256], BF16)) for _ in range(5)])
    outs = Ring([TB(A.alloc([128, NT], BF16)) for _ in range(2)])
    rcp = Ring([TB(A.alloc([128, NT], F32)) for _ in range(2)])
    pss = Ring([cx.psum[0], cx.psum[1], cx.psum[2], cx.psum[3]])
    pnd = Ring([cx.psum[4], cx.psum[5], cx.psum[6], cx.psum[7]])
    MEMSET(S_, "pool", qpad.t[:], 0.0, [qpad.b])
    wos_ = None
    if wo_staged:
        wos_ = WoStaged(cx, W)
        wos_.load(0)
    for c in range(4):
        S_.dma("sp", qpad.t[0:64, 0, :], W["qT0"][c * 128:c * 128 + 64, :], writes=[qpad.b])
        S_.dma("sp", qpad.t[64:128, 1, :], W["qT0"][c * 128 + 64:(c + 1) * 128, :], writes=[qpad.b])
        S_.dma("sp", kT.t[:], W["kT0"][c * 128:(c + 1) * 128, :], writes=[kT.b])
        MEMSET(S_, "pool", acc.t[:], 0.0, [acc.b])
        blocks = []
        for (win, d) in ((128, 1), (512, 4), (2048, 16)):
            L = Sx // d
            nb = L // 128
            for r in range(d):
                vb = vbs.next()
                src = W["v0p"][r:Sx:d, c * 256:(c + 1) * 256].rearrange("(kb j) e -> j kb e", j=128)
                loads = []
                for g0 in range(0, nb, 8):
                    g1 = min(nb, g0 + 8)
                    loads.append((vb.t[:, g0:g1, :, :].rearrange("p k h e -> p k (h e)"), src[:, g0:g1, :]))
                for kb in range(nb):
                    blocks.append((d, r, kb, nb, vb, loads if kb == 0 else None))

        def stage1(blk):
            d, r, kb, nb, vb, loads = blk
            if loads is not None:
                for (o_, i_) in loads:
                    S_.dma("sp", o_, i_, writes=[vb.b])
            nq = 256 if kb + 1 < nb else 128
            k0 = r + d * 128 * kb
            ksl = slice(k0, k0 + d * 127 + 1, d)
            qsl = slice(k0, k0 + d * (nq - 1) + 1, d)
            ps = pss.next()
            psv = ps.t[:, 0:2 * nq].rearrange("p (a n) -> p a n", a=2)
            MM(S_, psv, cx.ident.t[:], cx.mbias2.t[:, :, 0:nq], True, False, [cx.cb], [ps.b])
            MM(S_, psv, kT.t[:, ksl], qpad.t[:, :, qsl], False, True, [kT.b, qpad.b], [ps.b])
            pt = pts.next()
            ACT(S_, pt.t[:, :, 0:nq], psv, AF.Exp, [ps.b], [pt.b], scale=0.125)
            return (pt, nq, qsl, vb, kb)

        def stage2(st):
            pt, nq, qsl, vb, kb = st
            pnd_ = pnd.next()
            for hh in range(2):
                MM(S_, pnd_.t[:, 0:nq], vb.t[:, kb, hh, :], pt.t[:, hh, 0:nq], hh == 0, hh == 1,
                   [vb.b, pt.b], [pnd_.b])
            for hh in range(2):
                MM(S_, pnd_.t[:, 256:256 + nq], cx.onespad.t[:, hh, :], pt.t[:, hh, 0:nq], hh == 0, hh == 1,
                   [cx.cb, pt.b], [pnd_.b])
            TT(S_, "dve", acc.t[:, :, qsl], acc.t[:, :, qsl],
               pnd_.t[:].rearrange("p (a n) -> p a n", a=2)[:, :, 0:nq], ALU.add, [pnd_.b], [acc.b])

        LOOK = 2
        pend = []
        for blk in blocks:
            pend.append(stage1(blk))
            if len(pend) > LOOK:
                stage2(pend.pop(0))
        while pend:
            stage2(pend.pop(0))
        if c == 0 and after_pair0 is not None:
            after_pair0()
        if after_pair is not None:
            after_pair(c)
        if wos_ is not None:
            wos_.ops(c)
            if c + 1 < 4:
                wos_.load(c + 1)
        for ti in range(Sx // NT):
            tsl = slice(ti * NT, (ti + 1) * NT)
            rc = rcp.next()
            S_.op("dve", lambda e, o_=rc.t[:], i_=acc.t[:, 1, tsl]: e.reciprocal(out=o_, in_=i_), [acc.b], [rc.b])
            o = outs.next()
            TT(S_, "dve", o.t[:], acc.t[:, 0, tsl], rc.t[:], ALU.mult, [acc.b, rc.b], [o.b])
            S_.dma("pool", W["abT"][c * 128:(c + 1) * 128, tsl], o.t[:], reads=[o.b])


def emit_outproj(cx, xin, xout, actT, wo, wbuf, KC, ntiles):
    S_ = cx.S
    A = cx.sb
    A.reset()
    xin_v = xin.rearrange("(c p) s -> p c s", p=128)
    xout_v = xout.rearrange("(c p) s -> p c s", p=128)
    act_v = actT.rearrange("(c p) s -> p c s", p=128)
    xts = Ring([TB(A.alloc([128, 8, NT], F32), nb=8) for _ in range(2)])
    ats = Ring([TB(A.alloc([128, KC, NT], BF16)) for _ in range(2)])
    ws = Ring([TB(A.alloc([128, KC, 128], BF16)) for _ in range(3)])
    pr = Ring([cx.psum[0], cx.psum[1], cx.psum[2], cx.psum[3]])
    for ti in range(ntiles):
        tsl = slice(ti * NT, (ti + 1) * NT)
        xt = xts.next()
        S_.dma("sp", xt.t[:], xin_v[:, :, tsl], writes=xt.bs)
        at = ats.next()
        S_.dma("sp", at.t[:], act_v[:, :, tsl], writes=[at.b])
        for oc in range(8):
            w = ws.next()
            S_.dma("sp", w.t[:], wo[oc], reads=[wbuf], writes=[w.b])
            p = pr.next()
            for kc in range(KC):
                MM(S_, p.t[:], w.t[:, kc, :], at.t[:, kc, :], kc == 0, kc == KC - 1, [w.b, at.b], [p.b])
            TT(S_, "dve", xt.t[:, oc, :], p.t[:], xt.t[:, oc, :], ALU.add, [p.b], [xt.bs[oc]])
        S_.dma("pool", xout_v[:, :, tsl], xt.t[:], reads=xt.bs)


def emit_ret_proj(cx, xin, W, ntiles, after_tile0=None):
    S_ = cx.S
    A = cx.sb
    A.reset()
    xin_v = xin.rearrange("(c p) s -> p c s", p=128)
    Sx = ntiles * NT
    cos = TB(A.alloc([128, Sx], F32))
    sin = TB(A.alloc([128, Sx], F32))
    S_.dma("sp", cos.t[:], W["cos"][:, 0:Sx], writes=[cos.b])
    S_.dma("sp", sin.t[:], W["sin"][:, 0:Sx], writes=[sin.b])
    xts = Ring([TB(A.alloc([128, 8, NT], F32), nb=8) for _ in range(2)])
    cx.sq = TB(A.alloc([128, 8, NT], BF16))
    cx.rstd = Ring([TB(A.alloc([128, NT], F32)) for _ in range(2)])
    xns = Ring([TB(A.alloc([128, 8, NT], BF16), nb=8) for _ in range(2)])
    wr = Ring([TB(A.alloc([128, 8, 128], BF16)) for _ in range(4)])
    wsl = Ring([TB(A.alloc([128, 8, 512], BF16)) for _ in range(3)])
    t1s = Ring([TB(A.alloc([128, NT], F32)) for _ in range(3)])
    t2s = Ring([TB(A.alloc([128, NT], F32)) for _ in range(3)])
    t3s = Ring([TB(A.alloc([128, NT], F32)) for _ in range(3)])
    t4s = Ring([TB(A.alloc([128, NT], F32)) for _ in range(3)])
    ob = Ring([TB(A.alloc([128, NT], BF16)) for _ in range(6)])
    kTt = Ring([TB(A.alloc([128, 8, NT], BF16), nb=8) for _ in range(2)])
    kds = Ring([TB(A.alloc([128, 1024], BF16)) for _ in range(2)])
    obt = Ring([TB(A.alloc([128, 512], BF16)) for _ in range(3)])
    obf = Ring([TB(A.alloc([128, 512], F32)) for _ in range(2)])
    pr = Ring([cx.psum[0], cx.psum[1], cx.psum[2], cx.psum[3], cx.psum[4], cx.psum[7]])
    ptr = cx.psT
    ps_ss = cx.psum[6]

    def proj(oc, xn):
        w = wr.next()
        S_.dma("sp", w.t[:], W["r_wqk"][oc], reads=[W["b_rqk"]], writes=[w.b])
        p = pr.next()
        for kc in range(8):
            MM(S_, p.t[:], w.t[:, kc, :], xn.t[:, kc, :], kc == 0, kc == 7, [w.b, xn.bs[kc]], [p.b])
        return p

    for ti in range(ntiles):
        tsl = slice(ti * NT, (ti + 1) * NT)
        xt = xts.next()
        S_.dma("sp", xt.t[:], xin_v[:, :, tsl], writes=xt.bs)
        xn = xns.next()
        emit_rmsnorm_tile(cx, xt, lambda c: cx.gains.t[:, 1, c:c + 1], xn, float(D), ps_ss)
        kTb = kTt.next()
        for isk in (0, 1):
            for hd in range(4):
                p1 = proj(isk * 8 + 2 * hd, xn)
                p2 = proj(isk * 8 + 2 * hd + 1, xn)
                t1, t2, t3, t4 = t1s.next(), t2s.next(), t3s.next(), t4s.next()
                TT(S_, "dve", t1.t[:], p1.t[:], cos.t[:, tsl], ALU.mult, [p1.b, cos.b], [t1.b])
                TT(S_, "dve", t2.t[:], p2.t[:], sin.t[:, tsl], ALU.mult, [p2.b, sin.b], [t2.b])
                TT(S_, "dve", t3.t[:], p1.t[:], sin.t[:, tsl], ALU.mult, [p1.b, sin.b], [t3.b])
                TT(S_, "dve", t4.t[:], p2.t[:], cos.t[:, tsl], ALU.mult, [p2.b, cos.b], [t4.b])
                TT(S_, "dve", t1.t[:], t1.t[:], t2.t[:], ALU.subtract, [t2.b], [t1.b])
                TT(S_, "dve", t4.t[:], t3.t[:], t4.t[:], ALU.add, [t3.b], [t4.b])
                for half, rr in ((0, t1), (1, t4)):
                    ch = 2 * hd + half
                    r0 = ch * 128
                    if isk == 0:
                        o = ob.next()
                        ACT(S_, o.t[:], rr.t[:], AF.Copy, [rr.b], [o.b])
                        S_.dma("pool", W["r_qT"][r0:r0 + 128, tsl], o.t[:], reads=[o.b])
                        o2 = ob.next()
                        TT(S_, "dve", o2.t[:], rr.t[:], cx.qdec.t[:, hd, :], ALU.mult, [rr.b, cx.cb], [o2.b])
                        S_.dma("pool", W["r_qdT"][r0:r0 + 128, tsl], o2.t[:], reads=[o2.b])
                    else:
                        ACT(S_, kTb.t[:, ch, :], rr.t[:], AF.Copy, [rr.b], [kTb.bs[ch]], scale=1.0 / 16.0)
                        S_.dma("pool", W["r_kT"][r0:r0 + 128, tsl], kTb.t[:, ch, :], reads=[kTb.bs[ch]])
        for sub in range(4):
            for ch in range(8):
                TR(S_, ptr.t[:, ch * 128:(ch + 1) * 128], kTb.t[:, ch, sub * 128:(sub + 1) * 128], cx.ident.t[:],
                   [kTb.bs[ch], cx.cb], [ptr.b])
            kd = kds.next()
            for hd in range(4):
                S_.op("dve", lambda e, o_=kd.t[:, hd * 256:(hd + 1) * 256], i_=ptr.t[:, hd * 256:(hd + 1) * 256],
                      s_=cx.kdec.t[:, hd:hd + 1]: e.tensor_scalar_mul(out=o_, in0=i_, scalar1=s_),
                      [ptr.b, cx.cb], [kd.b])
            r0 = ti * NT + sub * 128
            S_.dma("pool", W["r_kd"][r0:r0 + 128, :], kd.t[:], reads=[kd.b])
        for which in (0, 1):
            for slab in range(4):
                w = wsl.next()
                S_.dma("sp", w.t[:], W["r_wvg"][which * 4 + slab], reads=[W["b_rvg"]], writes=[w.b])
                for sub in range(4):
                    p = pr.next()
                    for kc in range(8):
                        MM(S_, p.t[:], xn.t[:, kc, sub * 128:(sub + 1) * 128], w.t[:, kc, :], kc == 0, kc == 7,
                           [w.b, xn.bs[kc]], [p.b])
                    r0 = ti * NT + sub * 128
                    if which == 0:
                        o = obt.next()
                        ACT(S_, o.t[:], p.t[:], AF.Copy, [p.b], [o.b])
                        S_.dma("pool", W["r_v"][r0:r0 + 128, slab * 512:(slab + 1) * 512], o.t[:], reads=[o.b])
                    else:
                        o = obf.next()
                        ACT(S_, o.t[:], p.t[:], AF.Silu, [p.b], [o.b])
                        S_.dma("pool", W["r_g"][r0:r0 + 128, slab * 512:(slab + 1) * 512], o.t[:], reads=[o.b])
        if after_tile0 is not None:
            after_tile0(ti)


def emit_wo_scale(cx, W, reset=True):
    S_ = cx.S
    A = cx.sb
    if reset:
        A.reset()
    wf = Ring([TB(A.alloc([128, 16, 128], F32)) for _ in range(2)])
    wb = Ring([TB(A.alloc([128, 16, 128], BF16)) for _ in range(2)])
    for oc in range(8):
        f = wf.next()
        S_.dma("sp", f.t[:], W["r_wo_f"][oc], writes=[f.b])
        b = wb.next()
        for kc in range(16):
            S_.op("dve", lambda e, o_=b.t[:, kc, :], i_=f.t[:, kc, :], s_=cx.gng.t[:, kc:kc + 1]:
                  e.tensor_scalar_mul(out=o_, in0=i_, scalar1=s_), [f.b, cx.cb], [b.b])
        S_.dma("pool", W["r_wo"][oc], b.t[:], reads=[b.b], writes=[W["b_rwo"]])


class WoStaged:
    def __init__(self, cx, W):
        self.cx, self.W = cx, W
        A = cx.sb
        self.f = [TB(A.alloc([128, 16, 128], F32)) for _ in range(2)]
        self.b = [TB(A.alloc([128, 16, 128], BF16)) for _ in range(2)]

    def load(self, step):
        S_ = self.cx.S
        for i in range(2):
            S_.dma("sp", self.f[i].t[:], self.W["r_wo_f"][2 * step + i], writes=[self.f[i].b])

    def ops(self, step):
        S_, cx, W = self.cx.S, self.cx, self.W
        for i in range(2):
            f, b = self.f[i], self.b[i]
            for kc in range(16):
                S_.op("dve", lambda e, o_=b.t[:, kc, :], i_=f.t[:, kc, :], s_=cx.gng.t[:, kc:kc + 1]:
                      e.tensor_scalar_mul(out=o_, in0=i_, scalar1=s_), [f.b, cx.cb], [b.b])
            S_.dma("pool", W["r_wo"][2 * step + i], b.t[:], reads=[b.b], writes=[W["b_rwo"]])


def emit_ret_core(cx, xin, xout, W, nchunks, after_chunk=None):
    S_ = cx.S
    A = cx.sb
    A.reset()
    xin_v = xin.rearrange("(c p) s -> p c s", p=128)
    xout_v = xout.rearrange("(c p) s -> p c s", p=128)
    qT_v = W["r_qT"].rearrange("(c p) s -> p c s", p=128)
    qdT_v = W["r_qdT"].rearrange("(c p) s -> p c s", p=128)
    kT_v = W["r_kT"].rearrange("(c p) s -> p c s", p=128)
    wo = TB(A.alloc([128, 8, 16, 128], BF16))
    for oc in range(8):
        S_.dma("sp", wo.t[:, oc, :, :], W["r_wo"][oc], reads=[W["b_rwo"]], writes=[wo.b])
    stf = TB(A.alloc([128, 4, 2, 512], F32), nb=8)
    stb = TB(A.alloc([128, 4, 2, 512], BF16), nb=8)
    MEMSET(S_, "pool", stf.t[:], 0.0, stf.bs)
    MEMSET(S_, "pool", stb.t[:], 0.0, stb.bs)
    qs = Ring([TB(A.alloc([128, 8, 128], BF16)) for _ in range(2)])
    qds = Ring([TB(A.alloc([128, 8, 128], BF16)) for _ in range(2)])
    ks = Ring([TB(A.alloc([128, 8, 128], BF16)) for _ in range(2)])
    kds = Ring([TB(A.alloc([128, 1024], BF16)) for _ in range(2)])
    vs = Ring([TB(A.alloc([128, 2048], BF16)) for _ in range(2)])
    gs = Ring([TB(A.alloc([128, 2048], F32)) for _ in range(2)])
    xts = Ring([TB(A.alloc([128, 8, 128], F32)) for _ in range(2)])
    pts = Ring([TB(A.alloc([128, 4, 128], BF16)) for _ in range(2)])
    ofs = Ring([TB(A.alloc([128, 4, 512], F32), nb=4) for _ in range(2)])
    junk = TB(A.alloc([128, 512], F32))
    st1 = Ring([TB(A.alloc([128, 4], F32)) for _ in range(2)])
    st2 = Ring([TB(A.alloc([128, 4], F32)) for _ in range(2)])
    nmean = Ring([TB(A.alloc([128, 4], F32)) for _ in range(2)])
    var = Ring([TB(A.alloc([128, 4], F32)) for _ in range(2)])
    tmpy = Ring([TB(A.alloc([128, 512], F32)) for _ in range(2)])
    ytok = Ring([TB(A.alloc([128, 2048], BF16), nb=4) for _ in range(2)])
    yT = Ring([TB(A.alloc([128, 16, 128], BF16), nb=2) for _ in range(2)])
    ps_s = Ring([cx.psum[0]])
    ps_o = Ring([cx.psum[1], cx.psum[2]])
    ps_st = Ring([cx.psum[3], cx.psum[4]])
    ps_tr = cx.psT
    ps_op = Ring([cx.psum[6], cx.psum[7]])
    pending_tail = None
    tstate = {}
    for c in range(nchunks):
        csl = slice(c * 128, (c + 1) * 128)
        q, qd, k, kd, v, g, xt = qs.next(), qds.next(), ks.next(), kds.next(), vs.next(), gs.next(), xts.next()
        S_.dma("sp", q.t[:], qT_v[:, :, csl], writes=[q.b])
        S_.dma("sp", qd.t[:], qdT_v[:, :, csl], writes=[qd.b])
        S_.dma("sp", k.t[:], kT_v[:, :, csl], writes=[k.b])
        S_.dma("sp", kd.t[:], W["r_kd"][csl, :], writes=[kd.b])
        S_.dma("sp", v.t[:], W["r_v"][csl, :], writes=[v.b])
        S_.dma("sp", g.t[:], W["r_g"][csl, :], writes=[g.b])
        S_.dma("sp", xt.t[:], xin_v[:, :, csl], writes=[xt.b])
        of = ofs.next()
        s1, s2 = st1.next(), st2.next()
        MEMSET(S_, "dve", s1.t[:], 0.0, [s1.b])
        MEMSET(S_, "dve", s2.t[:], 0.0, [s2.b])
        pS = ps_s.next()
        for hd in range(4):
            MM(S_, pS.t[:, hd * 128:(hd + 1) * 128], k.t[:, 2 * hd, :], q.t[:, 2 * hd, :], True, False, [k.b, q.b], [pS.b])
            MM(S_, pS.t[:, hd * 128:(hd + 1) * 128], k.t[:, 2 * hd + 1, :], q.t[:, 2 * hd + 1, :], False, True,
               [k.b, q.b], [pS.b])
        pt = pts.next()
        TT(S_, "dve", pt.t[:], pS.t[:].rearrange("p (h n) -> p h n", h=4), cx.dmask.t[:], ALU.mult,
           [pS.b, cx.cb], [pt.b])
        for hd in range(4):
            vh = v.t[:, hd * 512:(hd + 1) * 512]
            pO = ps_o.next()
            MM(S_, pO.t[:], qd.t[:, 2 * hd, :], stb.t[:, hd, 0, :], True, False, [qd.b, stb.bs[2 * hd]], [pO.b])
            MM(S_, pO.t[:], qd.t[:, 2 * hd + 1, :], stb.t[:, hd, 1, :], False, False, [qd.b, stb.bs[2 * hd + 1]], [pO.b])
            MM(S_, pO.t[:], pt.t[:, hd, :], vh, False, True, [pt.b, v.b], [pO.b])
            ACT(S_, of.t[:, hd, :], pO.t[:], AF.Copy, [pO.b], [of.bs[hd], s1.b], accum_out=s1.t[:, hd:hd + 1])
            ACT(S_, junk.t[:], pO.t[:], AF.Square, [pO.b], [junk.b, s2.b], accum_out=s2.t[:, hd:hd + 1])
        if pending_tail is not None:
            pending_tail[0]()
        if c + 1 < nchunks:
            for hd in range(4):
                vh = v.t[:, hd * 512:(hd + 1) * 512]
                for j in range(2):
                    pZ = ps_st.next()
                    MM(S_, pZ.t[:], kd.t[:, (2 * hd + j) * 128:(2 * hd + j + 1) * 128], vh, True, True,
                       [kd.b, v.b], [pZ.b])
                    bi = 2 * hd + j
                    STT(S_, "dve", stf.t[:, hd, j, :], stf.t[:, hd, j, :], cx.cdec.t[:, hd:hd + 1], pZ.t[:],
                        ALU.mult, ALU.add, [pZ.b, cx.cb], [stf.bs[bi]])
            for hd in range(4):
                for j in range(2):
                    bi = 2 * hd + j
                    ACT(S_, stb.t[:, hd, j, :], stf.t[:, hd, j, :], AF.Copy, [stf.bs[bi]], [stb.bs[bi]])
        nm, vr = nmean.next(), var.next()
        S_.op("dve", lambda e, o_=nm.t[:], i_=s1.t[:]: e.tensor_scalar_mul(out=o_, in0=i_, scalar1=-1.0 / 512),
              [s1.b], [nm.b])
        TT(S_, "dve", vr.t[:], nm.t[:], nm.t[:], ALU.mult, [nm.b], [vr.b])
        STT(S_, "dve", vr.t[:], s2.t[:], 1.0 / 512, vr.t[:], ALU.mult, ALU.subtract, [s2.b], [vr.b])
        ACT(S_, vr.t[:], vr.t[:], AF.Ln, [vr.b, cx.cb], [vr.b], bias=cx.epsc.t[:, 0:1], scale=1.0)
        ACT(S_, vr.t[:], vr.t[:], AF.Exp, [vr.b], [vr.b], scale=-0.5)
        yk = ytok.next()
        for hd in range(4):
            ty = tmpy.next()
            STT(S_, "dve", ty.t[:], of.t[:, hd, :], nm.t[:, hd:hd + 1], g.t[:, hd * 512:(hd + 1) * 512],
                ALU.add, ALU.mult, [of.bs[hd], nm.b, g.b], [ty.b])
            S_.op("dve", lambda e, o_=yk.t[:, hd * 512:(hd + 1) * 512], i_=ty.t[:], s_=vr.t[:, hd:hd + 1]:
                  e.tensor_scalar_mul(out=o_, in0=i_, scalar1=s_), [ty.b, vr.b], [yk.bs[hd]])
        def tail_a(yk=yk, c=c):
            yt = yT.next()
            tstate[c] = yt
            for half in range(2):
                for i in range(8):
                    ch = half * 8 + i
                    TR(S_, ps_tr.t[:, i * 128:(i + 1) * 128], yk.t[:, ch * 128:(ch + 1) * 128], cx.ident.t[:],
                       [yk.bs[ch // 4], cx.cb], [ps_tr.b])
                ACT(S_, yt.t[:, half * 8:(half + 1) * 8, :], ps_tr.t[:].rearrange("p (c t) -> p c t", c=8), AF.Copy,
                    [ps_tr.b], [yt.bs[half]])

        def tail_b(xt=xt, csl=csl, c=c):
            yt = tstate.pop(c)
            for half in range(2):
                pP = ps_op.next()
                for o4 in range(4):
                    oc = half * 4 + o4
                    for kc in range(16):
                        MM(S_, pP.t[:, o4 * 128:(o4 + 1) * 128], wo.t[:, oc, kc, :], yt.t[:, kc, :], kc == 0, kc == 15,
                           [wo.b, yt.bs[kc // 8]], [pP.b])
                TT(S_, "dve", xt.t[:, half * 4:(half + 1) * 4, :], pP.t[:].rearrange("p (c t) -> p c t", c=4),
                   xt.t[:, half * 4:(half + 1) * 4, :], ALU.add, [pP.b], [xt.b])
            S_.dma("pool", xout_v[:, :, csl], xt.t[:], reads=[xt.b])

        if pending_tail is not None:
            pending_tail[1]()
        pending_tail = (tail_a, tail_b)
        if after_chunk is not None:
            after_chunk(c)
    if pending_tail is not None:
        pending_tail[0]()
        pending_tail[1]()


CF_LAYOUT = [("gains", 16), ("ffn_g", 16), ("ffn_cw", 264), ("qkg", 2), ("sconv", 12), ("kdec", 4),
             ("cdec", 4), ("gng", 16), ("dmask", 512), ("qdec", 2048),
             ("ones", 128), ("bdones", 128), ("onespad", 256), ("ident", 128), ("amask", 256), ("mbias", 256), ("amask2", 512), ("mbias2", 512)]
CF_OFF = {}
_o = 0
for _n, _w in CF_LAYOUT:
    CF_OFF[_n] = (_o, _w)
    _o += _w
NCF = _o


def build_program(Sx=S, phases="ABCDEFGH", out_name=None):
    from contextlib import ExitStack
    nc = bass.Bass("TRN2", target_bir_lowering=False)
    nt = Sx // NT

    def din(name, shape, dt=F32):
        return nc.dram_tensor(name, list(shape), dt, kind="ExternalInput").ap()

    def dint(name, shape, dt):
        return nc.dram_tensor(name, list(shape), dt, kind="Internal").ap()

    xT = din("xT", [D, Sx])
    cf = din("cf", [128, NCF])
    fw = {"l0_wfm": din("l0_wfm_f", [20, 128, 8, 128]), "l0_wv": din("l0_wv_f", [1, 128, 8, 512]),
          "l0_wo": din("l0_wo_f", [8, 128, 8, 128]),
          "ffn_up": din("ffn_up_f", [44, 128, 8, 256]), "ffn_dn": din("ffn_dn_f", [16, 128, 22, 128]),
          "r_wqk": din("r_wqk_f", [16, 128, 8, 128]), "r_wvg": din("r_wvg_f", [8, 128, 8, 512])}
    W = {"r_wo_f": din("r_wo_f", [8, 128, 16, 128]), "cos": din("cos", [128, Sx]), "sin": din("sin", [128, Sx])}
    yT = nc.dram_tensor("yT", [D, Sx], F32, kind="ExternalOutput").ap()
    bw = {k: dint(k + "_b", v.shape, BF16) for k, v in fw.items()}
    x1T = dint("x1T", [D, Sx], F32)
    x2T = dint("x2T", [D, Sx], F32)
    x3T = dint("x3T", [D, Sx], F32)
    W.update({"qT0": dint("qT0", [512, Sx], BF16), "kT0": dint("kT0", [512, Sx], BF16),
              "v0p": dint("v0p", [Sx, 1024], BF16), "abT": dint("abT", [D, Sx], BF16),
              "r_qT": dint("r_qT", [D, Sx], BF16), "r_qdT": dint("r_qdT", [D, Sx], BF16),
              "r_kT": dint("r_kT", [D, Sx], BF16), "r_kd": dint("r_kd", [Sx, D], BF16),
              "r_v": dint("r_v", [Sx, 2048], BF16), "r_g": dint("r_g", [Sx, 2048], F32),
              "r_wo": dint("r_wo", [8, 128, 16, 128], BF16)})
    cx = Ctx()
    cx.nc = nc
    cx.S = S_ = Sched(nc, {"sp": 28, "pool": 44})
    cx.sb = A = SbAlloc(nc)
    cx.C = None
    cx.consts_b = cx.cb = Buf(const=True)
    cfs = TB(A.alloc([128, NCF], F32, persist=True))
    S_.dma("sp", cfs.t[:], cf, writes=[cx.cb])

    class V:
        pass

    def fview(name, shape):
        o, w = CF_OFF[name]
        v = V()
        ap = cfs.t[:, o:o + w]
        if len(shape) == 2:
            v.t = ap.rearrange("p (a b) -> p a b", a=shape[0])
        elif len(shape) == 3:
            v.t = ap.rearrange("p (a b c) -> p a b c", a=shape[0], b=shape[1])
        else:
            v.t = ap
        v.b = cx.cb
        return v

    cx.gains = fview("gains", (2, 8))
    cx.ffn_g = fview("ffn_g", (2, 8))
    cx.ffn_cw = fview("ffn_cw", (2, 44, 3))
    cx.qkg = fview("qkg", ())
    cx.sconv = fview("sconv", (4, 3))
    cx.kdec = fview("kdec", ())
    cx.cdec = fview("cdec", ())
    cx.gng = fview("gng", ())
    cx.dmask = fview("dmask", (4, 128))
    cx.qdec = fview("qdec", (4, 512))
    cx.epsc = TB(A.alloc([128, 1], F32, persist=True))
    cx.epsc.b.const = True
    MEMSET(S_, "pool", cx.epsc.t[:], EPS, [cx.epsc.b])

    def bview(name, shape):
        o, w = CF_OFF[name]
        t = TB(A.alloc([128] + list(shape), BF16, persist=True))
        t.b = cx.cb
        src = cfs.t[:, o:o + w]
        if len(shape) == 2:
            src = src.rearrange("p (a b) -> p a b", a=shape[0])
        S_.op("dve", lambda e, o_=t.t[:], i_=src: e.tensor_copy(out=o_, in_=i_), [cx.cb], [Buf()])
        return t

    cx.ones = bview("ones", (128,))
    cx.bdones = bview("bdones", (128,))
    cx.onespad = bview("onespad", (2, 128))
    cx.ident = bview("ident", (128,))
    cx.amask = bview("amask", (256,))
    cx.mbias = bview("mbias", (256,))
    cx.amask2 = bview("amask2", (2, 256))
    cx.mbias2 = bview("mbias2", (2, 256))
    cx.psum = [TB(nc.alloc_psum_tensor("ps%d" % i, [128, 512], F32)) for i in range(8)]
    cx.psT = TB(cx.psum[5].t.bitcast(BF16))
    S_.barrier()
    bufs = {}

    def do_casts(keys):
        for k, lo, hi in keys:
            step = 4 if k in ("l0_wfm", "ffn_up", "r_wqk") else (2 if k in ("ffn_dn", "l0_wo", "r_wvg") else 1)
            bl = bufs.setdefault(k, [None] * fw[k].shape[0])
            for i in range(lo, hi, step):
                b = Buf(const=True)
                e_ = min(hi, i + step)
                S_.dma("pool", bw[k][i:e_], fw[k][i:e_], writes=[b], track=False)
                for j in range(i, e_):
                    bl[j] = b

    do_casts([("l0_wfm", 0, 20), ("l0_wv", 0, 1)])

    def spread(ti, keys, nparts=4, fine=False):
        if ti >= nparts:
            return
        for k, lo, hi in keys:
            step = 4 if k in ("l0_wfm", "ffn_up", "r_wqk") else (2 if k in ("ffn_dn", "l0_wo", "r_wvg") else 1)
            if fine:
                step = 1
            n = hi - lo
            per = -(-n // nparts)
            per = -(-per // step) * step
            a0 = lo + ti * per
            a1 = min(hi, a0 + per)
            if a0 < a1:
                do_casts([(k, a0, a1)])

    def ensure(keys):
        for k, lo, hi in keys:
            if k not in bufs:
                do_casts([(k, lo, hi)])
            else:
                for j in range(lo, hi):
                    if bufs[k][j] is None:
                        do_casts([(k, j, j + 1)])
    W.update({"l0_wfm": bw["l0_wfm"], "l0_wv": bw["l0_wv"][0], "l0_wo": bw["l0_wo"],
              "r_wqk": bw["r_wqk"], "r_wvg": bw["r_wvg"]})

    class AllBufs(Buf):
        pass

    def allb(k):
        b = Buf(const=True)
        b.w = [o for x in bufs[k] for o in x.w]
        return b

    class Lazy:
        const = True
        r = []

        def __init__(self, fn):
            self.fn = fn

        @property
        def w(self):
            return self.fn()

    W["b_l0"] = allb("l0_wv")
    W["bl_l0_wfm"] = bufs["l0_wfm"]
    W["b_l0o"] = Lazy(lambda: allb("l0_wo").w)
    W["b_rqk"] = Lazy(lambda: allb("r_wqk").w)
    W["b_rvg"] = Lazy(lambda: allb("r_wvg").w)
    W["b_rwo"] = Buf(const=True)
    cur = xT
    nxt = {"C": x1T, "D": x2T, "G": x3T, "H": yT}
    if "A" in phases:
        emit_l0_inproj(cx, xT, W, nt, after_tile0=lambda ti: spread(ti, [("l0_wo", 0, 8)]))
        S_.barrier()
    if "B" in phases:
        emit_l0_attn(cx, W, Sx, after_pair0=None, wo_staged=("E" in phases),
                     after_pair=lambda c: spread(c, [("ffn_up", 0, 22), ("ffn_dn", 0, 8)], nparts=3))
        S_.barrier()
    fuse_c = ("C" in phases) and ("D" in phases)
    if "C" in phases and not fuse_c:
        ensure([("l0_wo", 0, 8)])
        dst = yT if out_name == "C" else x1T
        emit_outproj(cx, cur, dst, W["abT"], W["l0_wo"], W["b_l0o"], 8, nt)
        cur = dst
        S_.barrier()
    if "D" in phases:
        ensure([("ffn_up", 0, 22), ("ffn_dn", 0, 8)])
        dst = yT if out_name == "D" else x2T
        b = Buf(const=True)
        b.w = [o for x in bufs["ffn_up"][0:22] + bufs["ffn_dn"][0:8] for o in x.w]
        if fuse_c:
            ensure([("l0_wo", 0, 8)])
        emit_ffn(cx, cur, dst, bw["ffn_up"][0:22], bw["ffn_dn"][0:8], 0, nt, wbufs=[b],
                 after_tile0=lambda ti: spread(ti, [("r_wqk", 0, 16), ("r_wvg", 0, 8)], nparts=7, fine=True),
                 pre=({"actT": W["abT"], "wo": W["l0_wo"], "wbuf": W["b_l0o"]} if fuse_c else None))
        cur = dst
        S_.barrier()
    if "E" in phases and "B" not in phases:
        emit_wo_scale(cx, W)
        S_.barrier()
    if "F" in phases:
        ensure([("r_wqk", 0, 16), ("r_wvg", 0, 8)])
        emit_ret_proj(cx, cur, W, nt)
        S_.barrier()
    if "G" in phases:
        dst = yT if out_name == "G" else x3T
        emit_ret_core(cx, cur, dst, W, Sx // 128,
                      after_chunk=lambda c: spread(c // 2, [("ffn_up", 22, 44), ("ffn_dn", 8, 16)], nparts=12, fine=True)
                      if c % 2 == 0 else None)
        cur = dst
        S_.barrier()
    if "H" in phases:
        ensure([("ffn_up", 22, 44), ("ffn_dn", 8, 16)])
        b = Buf(const=True)
        b.w = [o for x in bufs["ffn_up"][22:44] + bufs["ffn_dn"][8:16] for o in x.w]
        emit_ffn(cx, cur, yT, bw["ffn_up"][22:44], bw["ffn_dn"][8:16], 1, nt, wbufs=[b])
        S_.barrier()
    cx.S.barrier()
    with ExitStack() as stack:
        S_.emit(stack)
    return nc


def arr_fm(Wm):
    K, N = Wm.shape
    return np.ascontiguousarray(Wm.reshape(K // 128, 128, N // 128, 128).transpose(2, 1, 0, 3))


def arr_tm(Wm, slab=512):
    K, N = Wm.shape
    return np.ascontiguousarray(Wm.reshape(K // 128, 128, N // slab, slab).transpose(2, 1, 0, 3))


def host_consts(inp, Sx):
    f = np.float32
    cfd = {}
    g = np.zeros((128, 2, 8), f)
    g[:, 0, :] = inp["even_norm"][0].reshape(8, 128).T
    g[:, 1, :] = inp["odd_norm"][0].reshape(8, 128).T
    cfd["gains"] = g
    cfd["ffn_g"] = np.stack([inp["ffn_norm"][l].reshape(8, 128).T for l in range(2)], axis=1)
    cfd["ffn_cw"] = np.stack([inp["ffn_conv_w"][l].reshape(3, 44, 128).transpose(2, 1, 0) for l in range(2)], axis=1)
    cfd["qkg"] = np.stack([np.tile(inp["even_q_gain"][0], 2), np.tile(inp["even_k_gain"][0], 2)], axis=1)
    cfd["sconv"] = inp["even_sconv_w"][0].reshape(3, 4, 128).transpose(2, 1, 0)
    H = 4
    log_g = np.log1p(-(2.0 ** (-5.0 - np.arange(H, dtype=np.float64))))
    i = np.arange(128, dtype=np.float64)
    cfd["kdec"] = np.exp(log_g[None, :] * (127.0 - i[:, None]))
    cfd["cdec"] = np.tile(np.exp(log_g * 128.0)[None, :], (128, 1))
    cfd["gng"] = inp["ret_gn_gain"][0].reshape(16, 128).T
    diff = i[None, :] - i[:, None]
    dm = np.where(diff[None] >= 0, np.exp(log_g[:, None, None] * np.maximum(diff[None], 0.0)), 0.0)
    cfd["dmask"] = dm.transpose(1, 0, 2)
    qd = np.exp(log_g[:, None] * (i[None, :] + 1.0))
    cfd["qdec"] = np.tile(np.tile(qd, (1, 4))[None], (128, 1, 1))
    cfd["ones"] = np.ones((128, 128))
    bd = np.zeros((128, 128))
    bd[:64, :64] = 1
    bd[64:, 64:] = 1
    cfd["bdones"] = bd
    op = np.zeros((128, 2, 128))
    op[:, 0, :64] = 1
    op[:, 1, 64:] = 1
    cfd["onespad"] = op
    cfd["ident"] = np.eye(128)
    am = np.zeros((128, 256))
    j = np.arange(128)
    am[:, :128] = (j[None, :] >= j[:, None])
    am[:, 128:] = (j[None, :] <= j[:, None])
    cfd["amask"] = am
    cfd["mbias"] = (am - 1.0) * 30000.0
    cfd["amask2"] = np.stack([am, am], axis=1)
    cfd["mbias2"] = (cfd["amask2"] - 1.0) * 30000.0
    cf = np.zeros((128, NCF), f)
    for n, (o, w) in CF_OFF.items():
        cf[:, o:o + w] = np.asarray(cfd[n], dtype=np.float64).reshape(128, w)
    half = 128
    inv = 10000.0 ** (-np.arange(half, dtype=np.float64) / half)
    ang = inv[:, None] * np.arange(Sx, dtype=np.float64)[None, :]
    ang32 = (np.arange(Sx, dtype=f)[None, :] * (10000.0 ** (-np.arange(half, dtype=f) / half)).astype(f)[:, None]).astype(f)
    return cf, np.cos(ang32).astype(f), np.sin(ang32).astype(f)


def host_weights(inp):
    w_in = inp["even_w_in"][0]
    fm_cols = np.concatenate([w_in[:, 0:1024], w_in[:, 1536:3072]], axis=1)
    up = np.stack([inp["ffn_w_up"][l].reshape(8, 128, 2, 22, 128).transpose(3, 1, 0, 2, 4).reshape(22, 128, 8, 256)
                   for l in range(2)]).reshape(44, 128, 8, 256)
    dn = np.stack([arr_fm(inp["ffn_w_down"][l]) for l in range(2)]).reshape(16, 128, 22, 128)
    return {
        "l0_wfm_f": arr_fm(fm_cols),
        "l0_wv_f": arr_tm(w_in[:, 1024:1536]),
        "l0_wo_f": arr_fm(inp["even_w_out"][0]),
        "ffn_up_f": np.ascontiguousarray(up),
        "ffn_dn_f": np.ascontiguousarray(dn),
        "r_wqk_f": np.concatenate([arr_fm(inp["ret_wq"][0]), arr_fm(inp["ret_wk"][0])], axis=0),
        "r_wvg_f": np.concatenate([arr_tm(inp["ret_wv"][0]), arr_tm(inp["ret_wg"][0])], axis=0),
        "r_wo_f": arr_fm(inp["ret_wo"][0]),
    }


_CACHE = {}


def kernel(**inputs):
    inp = {k: np.asarray(v, dtype=np.float32) for k, v in inputs.items()}
    x = inp["x"]
    B = x.shape[0]
    if "nc" not in _CACHE:
        _CACHE["nc"] = build_program(S)
    nc = _CACHE["nc"]
    cf, cos, sin = host_consts(inp, S)
    hw = host_weights(inp)
    in_maps = []
    for b in range(B):
        m = {"xT": np.ascontiguousarray(x[b].T), "cf": cf, "cos": cos, "sin": sin}
        m.update(hw)
        in_maps.append(m)
    res = run_bass_kernel_spmd(nc, in_maps, core_ids=list(range(B)))
    out = np.stack([np.ascontiguousarray(res.results[b]["yT"].T) for b in range(B)], axis=0)
    return out.astype(np.float32)
```
